# Optimizing a Trainium2 kernel written in Bass

```python
import jax, jax.numpy as jnp
from jax import lax
import numpy as np

D_MODEL = 1024
BATCH = 2
SEQ = 8192
DEPTH = 1

HEAD_DIM = 64
ATTN_GROUPS = ((128, 1), (512, 4), (2048, 16))
ATTN_HEADS_PER_GROUP = 4
ATTN_HEADS = ATTN_HEADS_PER_GROUP * len(ATTN_GROUPS)
ATTN_WIDTH = ATTN_HEADS * HEAD_DIM
ATTN_OUT = ATTN_HEADS_PER_GROUP * HEAD_DIM
ROPE_THETA = 500000.0
ROPE_DIM = HEAD_DIM // 4

RWKV_HEADS = D_MODEL // HEAD_DIM
RWKV_WIDTH = RWKV_HEADS * HEAD_DIM
DECAY_LORA = 64
ICLR_LORA = 64
GATE_LORA = 160
RWKV_PROJ = 3 * RWKV_WIDTH + DECAY_LORA + ICLR_LORA + GATE_LORA

N_BRANCH = 2
GATE_WIDTH = N_BRANCH * D_MODEL
IN_WIDTH = 3 * ATTN_WIDTH + RWKV_PROJ + GATE_WIDTH

D_FF = ((8 * D_MODEL // 3 + 127) // 128) * 128
CONV_WIDTH = 3
RMS_EPS = 1e-6
GN_EPS = 64e-5

kernel_name = "hybrid_dilated_attn_rwkv7_convffn"


def rms_norm(x, g):
    xf = x.astype(jnp.float32)
    y = xf * lax.rsqrt(jnp.mean(xf * xf, axis=-1, keepdims=True) + RMS_EPS)
    return (y * g.astype(jnp.float32)).astype(x.dtype)


def partial_rotary(t, positions):
    half = ROPE_DIM // 2
    inv_freq = jnp.power(ROPE_THETA, -jnp.arange(0, ROPE_DIM, 2, dtype=jnp.float32) / ROPE_DIM)
    ang = positions.astype(jnp.float32)[:, None] * inv_freq[None, :]
    cos = jnp.cos(ang)[None, :, None, :]
    sin = jnp.sin(ang)[None, :, None, :]
    tf = t.astype(jnp.float32)
    t1, t2 = tf[..., :half], tf[..., half:ROPE_DIM]
    out = jnp.concatenate([t1 * cos - t2 * sin, t2 * cos + t1 * sin, tf[..., ROPE_DIM:]], axis=-1)
    return out.astype(t.dtype)


def dilated_window_attention(q, k, v, window, dilation):
    B, S, H, Dh = q.shape
    span = window // dilation
    L = span
    unit = dilation * L
    S_pad = -(-S // unit) * unit
    M = S_pad // dilation
    nb = M // L

    def to_blocks(t):
        t = jnp.pad(t.astype(jnp.float32), ((0, 0), (0, S_pad - S), (0, 0), (0, 0)))
        t = t.reshape(B, M, dilation, H, Dh).transpose(0, 2, 1, 3, 4)
        return t.reshape(B, dilation, nb, L, H, Dh)

    qb, kb, vb = to_blocks(q), to_blocks(k), to_blocks(v)

    def with_prev(t):
        prev = jnp.pad(t, ((0, 0), (0, 0), (1, 0), (0, 0), (0, 0), (0, 0)))[:, :, :-1]
        return jnp.concatenate([prev, t], axis=3)

    kc, vc = with_prev(kb), with_prev(vb)
    scores = jnp.einsum('brnqhd,brnkhd->brnhqk', qb, kc) * (Dh ** -0.5)
    qi = jnp.arange(L)[:, None]
    kj = jnp.arange(2 * L)[None, :]
    rel = L + qi - kj
    band = (rel >= 0) & (rel <= span)
    has_prev = (jnp.arange(nb) > 0)[:, None, None] | (kj >= L)[None]
    valid = band[None] & has_prev
    scores = jnp.where(valid[None, None, :, None], scores, -jnp.inf)
    m = jnp.max(scores, axis=-1, keepdims=True)
    e = jnp.exp(scores - m)
    den = jnp.sum(e, axis=-1, keepdims=True)
    o = jnp.einsum('brnhqk,brnkhd->brnqhd', e / den, vc)
    lse = (m + jnp.log(den))[..., 0]
    o = o.reshape(B, dilation, M, H, Dh).transpose(0, 2, 1, 3, 4).reshape(B, S_pad, H, Dh)[:, :S]
    lse = lse.transpose(0, 1, 2, 4, 3).reshape(B, dilation, M, H).transpose(0, 2, 1, 3)
    lse = lse.reshape(B, S_pad, H)[:, :S]
    return o, lse


def rwkv7_scan(r, decay, k, v, kk, a):
    B, S, H, N = r.shape

    def step(state, inp):
        r_t, w_t, k_t, v_t, kk_t, a_t = inp
        sa = jnp.einsum('bhvk,bhk->bhv', state, -kk_t)
        state = (state * w_t[:, :, None, :]
                 + sa[..., None] * (kk_t * a_t)[:, :, None, :]
                 + v_t[..., None] * k_t[:, :, None, :])
        y_t = jnp.einsum('bhvk,bhk->bhv', state, r_t)
        return state, y_t

    xs = tuple(jnp.swapaxes(t.astype(jnp.float32), 0, 1) for t in (r, decay, k, v, kk, a))
    s0 = jnp.zeros((B, H, N, N), jnp.float32)
    _, ys = lax.scan(step, s0, xs)
    return jnp.swapaxes(ys, 0, 1)


def rwkv7_branch(p, mu_shift, w0, w_decay_up, a0, w_a_up, w_g_up, k_k, k_a, r_k, ln_x_w, ln_x_b):
    B, S, _ = p.shape
    H, N = RWKV_HEADS, HEAD_DIM
    prev = jnp.pad(p, ((0, 0), (1, 0), (0, 0)))[:, :-1]
    p = p + (prev - p) * mu_shift
    c = np.cumsum([RWKV_WIDTH, RWKV_WIDTH, RWKV_WIDTH, DECAY_LORA, ICLR_LORA])
    r, k, v, zw, za, zg = jnp.split(p, [int(i) for i in c], axis=-1)
    w_log = -jax.nn.softplus(-(w0 + jnp.tanh(zw) @ w_decay_up)) - 0.5
    decay = jnp.exp(-jnp.exp(w_log.astype(jnp.float32)))
    a = jax.nn.sigmoid(a0 + za @ w_a_up)
    g = jax.nn.sigmoid(zg) @ w_g_up
    kk = (k * k_k).astype(jnp.float32).reshape(B, S, H, N)
    kk = kk / jnp.maximum(jnp.linalg.norm(kk, axis=-1, keepdims=True), 1e-12)
    k = k * (1.0 + (a - 1.0) * k_a)
    rh = r.astype(jnp.float32).reshape(B, S, H, N)
    kh = k.astype(jnp.float32).reshape(B, S, H, N)
    vh = v.astype(jnp.float32).reshape(B, S, H, N)
    ah = a.astype(jnp.float32).reshape(B, S, H, N)
    y = rwkv7_scan(rh, decay.reshape(B, S, H, N), kh, vh, kk, ah)
    mu = jnp.mean(y, axis=-1, keepdims=True)
    var = jnp.mean(jnp.square(y - mu), axis=-1, keepdims=True)
    y = (y - mu) * lax.rsqrt(var + GN_EPS)
    y = y * ln_x_w.astype(jnp.float32).reshape(H, N) + ln_x_b.astype(jnp.float32).reshape(H, N)
    y = y + jnp.sum(rh * kh * r_k.astype(jnp.float32), axis=-1, keepdims=True) * vh
    return (y.reshape(B, S, RWKV_WIDTH) * g.astype(jnp.float32)).astype(p.dtype)


def conv_ffn(h, w_ffn_up, conv_w, conv_b, w_ffn_down):
    u = h @ w_ffn_up
    u = lax.conv_general_dilated(
        u, conv_w, window_strides=(1,), padding=[(CONV_WIDTH - 1, 0)],
        dimension_numbers=('NWC', 'WIO', 'NWC'), feature_group_count=u.shape[-1]) + conv_b
    gate, val = jnp.split(u, 2, axis=-1)
    return (jax.nn.silu(gate) * val) @ w_ffn_down


def setup_inputs(seed: int = 0) -> dict:
    key = jax.random.key(seed)
    ks = jax.random.split(key, 26)
    f32 = jnp.float32

    def nrm(k, shape, scale):
        return jax.random.normal(k, shape, f32) * scale

    Lh = DEPTH
    return {
        "x": nrm(ks[0], (BATCH, SEQ, D_MODEL), 1.0),
        "norm_mix_g": 1.0 + nrm(ks[1], (Lh, D_MODEL), 0.02),
        "w_in": nrm(ks[2], (Lh, D_MODEL, IN_WIDTH), D_MODEL ** -0.5),
        "b_gate": nrm(ks[3], (Lh, GATE_WIDTH), 0.02),
        "mu_shift": jax.random.uniform(ks[4], (Lh, RWKV_PROJ), f32),
        "w0": -2.0 + nrm(ks[5], (Lh, RWKV_WIDTH), 1.0),
        "w_decay_up": nrm(ks[6], (Lh, DECAY_LORA, RWKV_WIDTH), 0.1 * DECAY_LORA ** -0.5),
        "a0": nrm(ks[7], (Lh, RWKV_WIDTH), 0.5),
        "w_a_up": nrm(ks[8], (Lh, ICLR_LORA, RWKV_WIDTH), ICLR_LORA ** -0.5),
        "w_g_up": nrm(ks[9], (Lh, GATE_LORA, RWKV_WIDTH), GATE_LORA ** -0.5),
        "k_k": 0.85 + nrm(ks[10], (Lh, RWKV_WIDTH), 0.02),
        "k_a": 1.0 + nrm(ks[11], (Lh, RWKV_WIDTH), 0.02),
        "r_k": nrm(ks[12], (Lh, RWKV_HEADS, HEAD_DIM), 0.1),
        "ln_x_w": 1.0 + nrm(ks[13], (Lh, RWKV_WIDTH), 0.02),
        "ln_x_b": nrm(ks[14], (Lh, RWKV_WIDTH), 0.02),
        "w_branch_attn": nrm(ks[15], (Lh, ATTN_OUT, D_MODEL), ATTN_OUT ** -0.5),
        "w_branch_rwkv": nrm(ks[16], (Lh, RWKV_WIDTH, D_MODEL), RWKV_WIDTH ** -0.5),
        "w_out": nrm(ks[17], (Lh, D_MODEL, D_MODEL), D_MODEL ** -0.5),
        "norm_ffn_g": 1.0 + nrm(ks[18], (Lh, D_MODEL), 0.02),
        "w_ffn_up": nrm(ks[19], (Lh, D_MODEL, 2 * D_FF), D_MODEL ** -0.5),
        "conv_w": nrm(ks[20], (Lh, CONV_WIDTH, 1, 2 * D_FF), CONV_WIDTH ** -0.5),
        "conv_b": nrm(ks[21], (Lh, 2 * D_FF), 0.02),
        "w_ffn_down": nrm(ks[22], (Lh, D_FF, D_MODEL), D_FF ** -0.5),
        "norm_final_g": 1.0 + nrm(ks[23], (D_MODEL,), 0.02),
    }


def reference(x, norm_mix_g, w_in, b_gate, mu_shift, w0, w_decay_up, a0, w_a_up, w_g_up,
              k_k, k_a, r_k, ln_x_w, ln_x_b, w_branch_attn, w_branch_rwkv, w_out,
              norm_ffn_g, w_ffn_up, conv_w, conv_b, w_ffn_down, norm_final_g):
    B, S, _ = x.shape
    positions = jnp.arange(S, dtype=jnp.int32)
    for l in range(DEPTH):
        h = rms_norm(x, norm_mix_g[l])
        proj = h @ w_in[l]
        q = proj[..., :ATTN_WIDTH].reshape(B, S, ATTN_HEADS, HEAD_DIM)
        k = proj[..., ATTN_WIDTH:2 * ATTN_WIDTH].reshape(B, S, ATTN_HEADS, HEAD_DIM)
        v = proj[..., 2 * ATTN_WIDTH:3 * ATTN_WIDTH].reshape(B, S, ATTN_HEADS, HEAD_DIM)
        p_rwkv = proj[..., 3 * ATTN_WIDTH:3 * ATTN_WIDTH + RWKV_PROJ]
        gate_logits = proj[..., 3 * ATTN_WIDTH + RWKV_PROJ:] + b_gate[l]

        q = partial_rotary(q, positions)
        k = partial_rotary(k, positions)
        outs, lses = [], []
        for gi, (window, dilation) in enumerate(ATTN_GROUPS):
            sl = slice(gi * ATTN_HEADS_PER_GROUP, (gi + 1) * ATTN_HEADS_PER_GROUP)
            o_g, lse_g = dilated_window_attention(q[:, :, sl], k[:, :, sl], v[:, :, sl], window, dilation)
            outs.append(o_g)
            lses.append(lse_g)
        wts = jax.nn.softmax(jnp.stack(lses, axis=0), axis=0)
        y_attn = jnp.sum(wts[..., None] * jnp.stack(outs, axis=0), axis=0)
        y_attn = y_attn.reshape(B, S, ATTN_OUT).astype(x.dtype)

        y_rwkv = rwkv7_branch(p_rwkv, mu_shift[l], w0[l], w_decay_up[l], a0[l], w_a_up[l], w_g_up[l],
                              k_k[l], k_a[l], r_k[l], ln_x_w[l], ln_x_b[l])

        g_attn, g_rwkv = jnp.split(jax.nn.sigmoid(gate_logits), 2, axis=-1)
        merged = g_attn * (y_attn @ w_branch_attn[l]) + g_rwkv * (y_rwkv @ w_branch_rwkv[l])
        x = x + merged @ w_out[l]

        x = x + conv_ffn(rms_norm(x, norm_ffn_g[l]), w_ffn_up[l], conv_w[l], conv_b[l], w_ffn_down[l])
    return rms_norm(x, norm_final_g)
```

```python
import contextlib
import numpy as np
import ml_dtypes
import concourse.bass as bass
import concourse.mybir as mybir
from concourse.bass_utils import run_bass_kernel_spmd

F32 = mybir.dt.float32
BF16 = mybir.dt.bfloat16
ALU = mybir.AluOpType
AF = mybir.ActivationFunctionType

COMPUTE = ("tensor", "vector", "scalar", "gpsimd")
ALLENG = ("sync", "tensor", "vector", "scalar", "gpsimd")

S = 8192
TQ = 2050
TQP = 2064
NT4 = 410
DFF = 2816


class Buf:
    __slots__ = ("name", "writers", "readers", "war", "sems")

    def __init__(self, name=""):
        self.name = name
        self.writers = {}
        self.readers = {}
        self.war = {}
        self.sems = {}


class K:
    def __init__(self, nc, stack):
        self.nc = nc
        self.stack = stack
        self.ops = {e: [] for e in ALLENG}
        self.count = {e: 0 for e in COMPUTE}
        self.esem = {e: stack.enter_context(nc.semaphore("sem_" + e)) for e in COMPUTE}
        self.waited = {e: {} for e in ALLENG}
        self.nsem = 0
        self.dbufs = []
        self.freesems = {}
        self.rawtot = {}

    def newsem(self, name):
        self.nsem += 1
        return self.stack.enter_context(self.nc.semaphore(name))

    def _collect(self, eng, reads, writes, joins):
        deps = {}

        def add(d):
            for sem, (val, en) in d.items():
                if eng == "tensor" and en == "tensor":
                    continue
                if sem not in deps or deps[sem] < val:
                    deps[sem] = val
        for b in reads:
            add(b.writers)
        for b in writes:
            add(b.writers)
            add(b.readers)
        for b in joins:
            add(b.war)
        w = self.waited[eng]
        out = []
        for sem, val in deps.items():
            if w.get(sem, 0) < val:
                w[sem] = val
                out.append((sem, val))
        return out

    def _commit(self, ev, reads, writes, joins):
        sem, val, en = ev
        for b in reads:
            b.readers[sem] = (val, en)
        for b in writes:
            war = dict(b.writers)
            for s, v in b.readers.items():
                if s not in war or war[s][0] < v[0]:
                    war[s] = v
            b.war = war
            b.writers = {sem: (val, en)}
            b.readers = {}
        for b in joins:
            b.writers[sem] = (val, en)

    def op(self, eng, fn, reads=(), writes=(), joins=()):
        waits = self._collect(eng, reads, writes, joins)
        self.count[eng] += 1
        n = self.count[eng]
        sem = self.esem[eng]
        self._commit((sem, n, eng), reads, writes, joins)
        self.ops[eng].append((waits, fn, (sem, 1)))

    def dma(self, q, out_ap, in_ap, src, dst, owner, join=False):
        reads = [src]
        writes = [] if join else [dst]
        joins = [dst] if join else []
        waits = self._collect(q, reads, writes, joins)
        kind = "sw" if q == "gpsimd" else "hw"
        if kind not in owner.sems:
            fs = self.freesems.setdefault(kind, [])
            if fs:
                owner.sems[kind] = list(fs.pop())
            else:
                owner.sems[kind] = [self.newsem("dsem_%s%d" % (kind, self.nsem)), 0]
            if owner not in self.dbufs:
                self.dbufs.append(owner)
        st_ = owner.sems[kind]
        st_[1] += 16
        ev = (st_[0], st_[1], "dma")
        self._commit(ev, reads, writes, joins)

        def fn(e, out_ap=out_ap, in_ap=in_ap):
            return e.dma_start(out=out_ap, in_=in_ap)
        self.ops[q].append((waits, fn, (st_[0], 16)))

    def raw(self, eng, fn, reads, writes, sem, inc=1):
        waits = self._collect(eng, reads, writes, ())
        tot = self.rawtot.get(sem, 0) + inc
        self.rawtot[sem] = tot
        self._commit((sem, tot, "raw"), reads, writes, ())
        self.ops[eng].append((waits, fn, (sem, inc)))

    def barrier(self, retire=True):
        allev = {}
        for e in COMPUTE:
            if self.count[e]:
                allev[self.esem[e]] = self.count[e]
        for b in self.dbufs:
            for sem_, tot_ in b.sems.values():
                allev[sem_] = max(allev.get(sem_, 0), tot_)
        for sem, tot in self.rawtot.items():
            allev[sem] = tot
        for e in ALLENG:
            w = self.waited[e]
            waits = []
            for sem, val in allev.items():
                if w.get(sem, 0) < val:
                    w[sem] = val
                    waits.append((sem, val))
            self.ops[e].append((waits, None, None))
        if retire:
            for b in self.dbufs:
                for kind, (sem_, tot_) in b.sems.items():
                    self.freesems.setdefault(kind, []).append((sem_, tot_))
                b.sems = {}
            self.dbufs = []

    def emit(self):
        nc = self.nc
        with nc.Block() as block:
            def run(e, name):
                embed_ok = name in ("vector", "scalar", "gpsimd")
                for waits, fn, inc in self.ops[name]:
                    is_compute = fn is not None and inc is not None and name in self.esem and inc[0] is self.esem[name]
                    if embed_ok and is_compute and waits:
                        for sem, val in waits[1:]:
                            e.wait_ge(sem, val)
                        ins = fn(e)
                        ins._wait_ge(waits[0][0], waits[0][1])
                        ins.then_inc(inc[0], inc[1])
                        continue
                    for sem, val in waits:
                        e.wait_ge(sem, val)
                    if fn is not None:
                        ins = fn(e)
                        if inc is not None:
                            ins.then_inc(inc[0], inc[1])

            @block.sync
            def _(e):
                run(e, "sync")

            @block.tensor
            def _(e):
                run(e, "tensor")

            @block.vector
            def _(e):
                run(e, "vector")

            @block.scalar
            def _(e):
                run(e, "scalar")

            @block.gpsimd
            def _(e):
                run(e, "gpsimd")


class B:
    def __init__(self, nc, st):
        self.nc = nc
        self.k = K(nc, st)
        self.st = st
        self.ext = Buf("ext")
        self.pt = []
        self.pb = []
        for i in range(8):
            self.pt.append(st.enter_context(nc.psum_tensor("ps%d" % i, [128, 512], F32)))
            self.pb.append(Buf("ps%d" % i))
        self.pi = 0
        self.nring = 8

    def psum(self):
        i = self.pi % self.nring
        self.pi = (i + 1) % self.nring
        return self.pt[i], self.pb[i]

    def din(self, name, shape, dt=F32):
        return self.nc.dram_tensor(name, list(shape), dt, kind="ExternalInput").ap()

    def dout(self, name, shape, dt=F32):
        return self.nc.dram_tensor(name, list(shape), dt, kind="ExternalOutput").ap()

    def dscr(self, name, shape, dt):
        return self.nc.dram_tensor(name, list(shape), dt).ap()

    def sb(self, stack, name, shape, dt):
        return stack.enter_context(self.nc.sbuf_tensor(name, list(shape), dt))

    def mm(self, out, lhsT, rhs, start, stop, reads, wb):
        self.k.op("tensor", lambda e: e.matmul(out, lhsT=lhsT, rhs=rhs, start=start, stop=stop),
                  reads=reads, writes=[wb])

    def act(self, out, in_, func, reads, writes, bias=None, scale=None, eng="scalar", joins=()):
        kw = {}
        if bias is not None:
            kw["bias"] = bias
        if scale is not None:
            kw["scale"] = scale
        self.k.op(eng, lambda e: e.activation(out=out, in_=in_, func=func, **kw), reads=reads, writes=writes, joins=joins)

    def tt(self, eng, out, in0, in1, op, reads, writes, joins=()):
        self.k.op(eng, lambda e: e.tensor_tensor(out=out, in0=in0, in1=in1, op=op), reads=reads, writes=writes, joins=joins)

    def ts(self, eng, out, in0, s1, s2, op0, op1, reads, writes, joins=()):
        if s2 is None:
            self.k.op(eng, lambda e: e.tensor_scalar(out=out, in0=in0, scalar1=s1, scalar2=None, op0=op0), reads=reads, writes=writes, joins=joins)
        else:
            self.k.op(eng, lambda e: e.tensor_scalar(out=out, in0=in0, scalar1=s1, scalar2=s2, op0=op0, op1=op1), reads=reads, writes=writes, joins=joins)

    def stt(self, out, in0, scalar, in1, op0, op1, reads, writes, joins=()):
        self.k.op("vector", lambda e: e.scalar_tensor_tensor(out=out, in0=in0, scalar=scalar, in1=in1, op0=op0, op1=op1),
                  reads=reads, writes=writes, joins=joins)

    def cp(self, eng, out, in_, reads, writes, joins=()):
        if eng == "scalar":
            self.k.op(eng, lambda e: e.copy(out=out, in_=in_), reads=reads, writes=writes, joins=joins)
        else:
            self.k.op(eng, lambda e: e.tensor_copy(out=out, in_=in_), reads=reads, writes=writes, joins=joins)

    def memset(self, eng, ap, val, wb):
        self.k.op(eng, lambda e: e.memset(ap, val), reads=(), writes=[wb])

    def ld(self, out_ap, in_ap, dst, src=None, q="sync", join=False):
        self.k.dma(q, out_ap, in_ap, src if src is not None else self.ext, dst, dst, join=join)

    def stx(self, out_ap, in_ap, src, dst, q="gpsimd", join=False):
        self.k.dma(q, out_ap, in_ap, src, dst, src, join=join)


QK_C0 = 0
V_C0 = 768
RW_C0 = 960
NW1 = 2016
RW_TILES = [(0, 128), (128, 128), (256, 128), (384, 128), (512, 128), (640, 128),
            (768, 64), (832, 64), (896, 128), (1024, 32)]
DIL = (1, 4, 16)


def rmsnorm_sub(b, xs, bxs, N, sq, bsq, onesm, bones, rstd, brstd, eps, outfn, bout, first_out):
    b.act(sq[:, :, :N], xs, AF.Square, reads=[bxs], writes=[bsq])
    ps, pbuf = b.psum()
    for kc in range(8):
        b.mm(ps[:, :N], onesm[:], sq[:, kc, :N], kc == 0, kc == 7, [bsq, bones], pbuf)
    b.act(rstd[:, :N], ps[:, :N], AF.Ln, reads=[pbuf, b.beps], writes=[brstd], bias=b.epst[:, 0:1] if eps == 1e-6 else b.epst[:, 1:2])
    b.act(rstd[:, :N], rstd[:, :N], AF.Exp, reads=[brstd], writes=[brstd], scale=-0.5)
    for kc in range(8):
        eng = "gpsimd" if kc in (1, 4, 6) else "vector"
        if kc == 0 and first_out:
            b.tt(eng, outfn(kc), xs[:, kc, :], rstd[:, :N], ALU.mult, reads=[bxs, brstd], writes=[bout])
        else:
            b.tt(eng, outfn(kc), xs[:, kc, :], rstd[:, :N], ALU.mult, reads=[bxs, brstd], writes=(), joins=[bout])


def load_weight_bf16(b, ph, dst, bdst, src, nrow_chunks, ncols, gvec=None, stage=None, bstage=None, rows=128, gbuf=None):
    for kc in range(nrow_chunks):
        b.ld(stage[:rows, :ncols], src[kc * rows:(kc + 1) * rows, :], bstage)
        eng = "vector" if kc % 2 == 0 else "gpsimd"
        if gvec is not None:
            b.ts(eng, dst[:rows, kc, :ncols], stage[:rows, :ncols], gvec[:rows, kc:kc + 1], None, ALU.mult, None,
                 reads=[bstage, gbuf], writes=(), joins=[bdst])
        else:
            b.cp(eng, dst[:rows, kc, :ncols], stage[:rows, :ncols], reads=[bstage], writes=(), joins=[bdst])


def build(upto=9, dbg=()):
    nc = bass.Bass("TRN2", target_bir_lowering=False)
    st = contextlib.ExitStack()
    with st:
        b = B(nc, st)
        k = b.k
        xT = b.din("xT", [1024, S])
        xo = b.din("xo", [1024, TQP])
        w1 = b.din("w1", [1024, NW1])
        gv = b.din("gv", [128, 24])
        mu = b.din("mu", [128, 10])
        cst = b.din("cst", [128, S])
        snt = b.din("snt", [128, S])
        vec = b.din("vec", [128, 2, 8])
        wd = b.din("wd", [64, 256])
        wa = b.din("wa", [64, 256])
        wg = b.din("wg", [160, 256])
        wba = b.din("wba", [64, 1024])
        wbr = b.din("wbr", [256, 1024])
        wgate = b.din("wgate", [1024, 2048])
        bgate = b.din("bgate", [128, 16])
        wout = b.din("wout", [1024, 1024])
        wup = b.din("wup", [1024, 2 * DFF])
        cw = b.din("cw", [128, 44, 3])
        cb = b.din("cb", [128, 44])
        wdn = b.din("wdn", [DFF, 1024])
        msk = b.din("msk", [128, 5, 128])
        oT = b.dout("oT", [1024, 2048])
        dbo = {}
        for name, shape, dt in dbg:
            dbo[name] = b.dout("dbg_" + name, shape, dt)
        qk = b.dscr("qk", [6, 64, S], BF16)
        vv = b.dscr("vv", [3, 64, 128, 64], BF16)
        rw = b.dscr("rw", [1280, S], F32)
        dsc = b.dscr("dsc", [64, S], F32)
        cin = b.dscr("cin", [4 * 1024, TQP], BF16)
        cout = b.dscr("cout", [1024, TQP], BF16)
        cinR = b.dscr("cinR", [4 * 1024, TQP], BF16)
        coutR = b.dscr("coutR", [1024, TQP], BF16)
        Bqk, Bvv, Brw, Bdsc, Bcin, Bcout, BoT = [Buf(n) for n in "qk vv rw dsc cin cout oT".split()]
        BcinR, BcoutR = Buf("cinR"), Buf("coutR")
        ccsem = k.newsem("ccsem")
        ccsemR = k.newsem("ccsemR")
        onesm = b.sb(st, "onesm", [128, 128], BF16)
        bones = Buf("ones")
        b.memset("vector", onesm[:], 1.0 / 1024.0, bones)
        b.epst = b.sb(st, "epst", [128, 4], F32)
        b.beps = Buf("eps")
        b.memset("vector", b.epst[:, 0:1], 1e-6, b.beps)
        b.memset("vector", b.epst[:, 1:2], 64e-5, b.beps)
        b.memset("vector", b.epst[:, 2:3], 1e-24, b.beps)
        b.memset("vector", b.epst[:, 3:4], 0.0, b.beps)
        gvs = b.sb(st, "gvs", [128, 24], F32)
        bgvs = Buf("gvs")
        b.ld(gvs[:], gv, bgvs)
        mskf = b.sb(st, "mskf", [128, 5, 128], F32)
        mskb = b.sb(st, "mskb", [128, 5, 128], BF16)
        bmskf, bmsk = Buf("mskf"), Buf("msk")
        b.ld(mskf[:], msk, bmskf)
        b.cp("vector", mskb[:], mskf[:], [bmskf], [bmsk])

        with contextlib.ExitStack() as ph:
            W1 = b.sb(ph, "W1", [128, 8, NW1], BF16)
            bW1 = Buf("W1")
            wsts_ = [b.sb(ph, "wst%d" % i, [128, NW1], F32) for i in range(3)]
            bwsts_ = [Buf("wst%d" % i) for i in range(3)]
            for kc in range(8):
                w_, bw_ = wsts_[kc % 3], bwsts_[kc % 3]
                b.ld(w_[:, :], w1[kc * 128:(kc + 1) * 128, :], bw_)
                if kc % 2 == 0:
                    b.ts("vector", W1[:, kc, :], w_[:, :], gvs[:, kc:kc + 1], None, ALU.mult, None, [bw_, bgvs], (), joins=[bW1])
                else:
                    b.act(W1[:, kc, :], w_[:, :], AF.Copy, [bw_, bgvs], (), scale=gvs[:, kc:kc + 1], joins=[bW1])
            mus = b.sb(ph, "mus", [128, 10], F32)
            bmus = Buf("mus")
            b.ld(mus[:], mu, bmus)
            xs = [b.sb(ph, "xs%d" % i, [128, 8, 512], F32) for i in range(2)]
            bxs = [Buf("xs%d" % i) for i in range(2)]
            sq = b.sb(ph, "sq", [128, 8, 512], BF16)
            bsq = Buf("sq")
            rstd = b.sb(ph, "rstd", [128, 512], F32)
            brstd = Buf("rstd")
            hT = b.sb(ph, "hT", [128, 8, 2048], BF16)
            bhT = Buf("hT")
            cs4 = b.sb(ph, "cs4", [128, 2, 2048], F32)
            bcs4 = Buf("cs4")
            Pt = [b.sb(ph, "P%d" % i, [128, 513], F32) for i in range(10)]
            bP = [Buf("P%d" % i) for i in range(10)]
            for i in range(10):
                b.memset("gpsimd", Pt[i][:, 0:513], 0.0, bP[i])
            t1 = b.sb(ph, "t1", [128, 512], F32)
            t2 = b.sb(ph, "t2", [128, 512], F32)
            t3 = b.sb(ph, "t3", [128, 512], F32)
            bt1, bt2, bt3 = Buf("t1"), Buf("t2"), Buf("t3")
            oq = [b.sb(ph, "oq%d" % i, [128, 512], BF16) for i in range(2)]
            boq = [Buf("oq%d" % i) for i in range(2)]
            tr = b.sb(ph, "tr", [128, 512], F32)
            btr = Buf("tr")
            orw = [b.sb(ph, "orw%d" % i, [128, 512], F32) for i in range(3)]
            borw = [Buf("orw%d" % i) for i in range(3)]
            vst = [b.sb(ph, "vst%d" % i, [128, 8, 64], BF16) for i in range(2)]
            bvst = [Buf("vst%d" % i) for i in range(2)]
            nsub = 0
            noq = 0
            norw = 0
            nvst = 0
            for tt_ in range(4):
                for sub in range(4):
                    t0 = tt_ * 2048 + sub * 512
                    xb, bxb = xs[nsub % 2], bxs[nsub % 2]
                    nsub += 1
                    b.ld(xb[:], xT.rearrange("(k p) t -> p k t", p=128)[:, :, t0:t0 + 512], bxb)
                    rmsnorm_sub(b, xb[:], bxb, 512, sq, bsq, onesm, bones, rstd, brstd, 1e-6,
                                lambda kc, sub=sub: hT[:, kc, sub * 512:(sub + 1) * 512], bhT, sub == 0)
                T0_ = tt_ * 2048
                b.ld(cs4[:, 0, :], cst[:, T0_:T0_ + 2048], bcs4)
                b.ld(cs4[:, 1, :], snt[:, T0_:T0_ + 2048], bcs4, join=True)
                for sp_ in range(2):
                    subs = (2 * sp_, 2 * sp_ + 1)
                    for (c0, n, outs_) in ((0, 128, (0, 2)), (256, 128, (1, 3)), (512, 64, (4,)), (640, 64, (5,))):
                        pas = [b.psum() for _ in subs]
                        for kc in range(8):
                            for si, sub in enumerate(subs):
                                b.mm(pas[si][0][0:n, :], W1[:, kc, c0:c0 + n], hT[:, kc, sub * 512:(sub + 1) * 512], kc == 0, kc == 7,
                                     [bW1, bhT], pas[si][1])
                        pcs = [b.psum() for _ in subs]
                        for kc in range(8):
                            for si, sub in enumerate(subs):
                                b.mm(pcs[si][0][0:n, :], W1[:, kc, c0 + n:c0 + 2 * n], hT[:, kc, sub * 512:(sub + 1) * 512], kc == 0, kc == 7,
                                     [bW1, bhT], pcs[si][1])
                        for si, sub in enumerate(subs):
                            t0 = T0_ + sub * 512
                            ssl = slice(sub * 512, (sub + 1) * 512)
                            pa, bpa = pas[si]
                            pc, bpc = pcs[si]
                            b.tt("vector", t1[0:n, :], pa[0:n, :], cs4[0:n, 0, ssl], ALU.mult, [bpa, bcs4], [bt1])
                            b.cp("scalar", t2[0:n, :], pc[0:n, :], [bpc], [bt2])
                            b.tt("gpsimd", t3[0:n, :], t2[0:n, :], cs4[0:n, 1, ssl], ALU.mult, [bt2, bcs4], [bt3])
                            o, bo = oq[noq % 2], boq[noq % 2]
                            noq += 1
                            b.tt("vector", o[0:n, :], t1[0:n, :], t3[0:n, :], ALU.add, [bt1, bt3], [bo])
                            for oi, idx in enumerate(outs_):
                                b.stx(qk[idx, :, t0:t0 + 512], o[oi * 64:(oi + 1) * 64, :], bo, Bqk, join=True)
                for ct, (co, n) in enumerate(RW_TILES):
                    c0 = RW_C0 + co
                    pas = [b.psum() for _ in range(4)]
                    for kc in range(8):
                        for sub in range(4):
                            b.mm(pas[sub][0][0:n, :], W1[:, kc, c0:c0 + n], hT[:, kc, sub * 512:(sub + 1) * 512], kc == 0, kc == 7,
                                 [bW1, bhT], pas[sub][1])
                    P, bPc = Pt[ct], bP[ct]
                    for sub in range(4):
                        t0 = T0_ + sub * 512
                        pa, bpa = pas[sub]
                        b.cp("gpsimd", P[:n, 0:1], P[:n, 512:513], [bPc], [bPc])
                        b.cp("scalar", P[:n, 1:513], pa[0:n, :], [bpa], [bPc])
                        b.tt("vector", tr[:n, :], P[:n, 0:512], P[:n, 1:513], ALU.subtract, [bPc], [btr])
                        o, bo = orw[norw % 3], borw[norw % 3]
                        norw += 1
                        b.stt(o[:n, :], tr[:n, :], mus[:n, ct:ct + 1], P[:n, 1:513], ALU.mult, ALU.add, [btr, bmus, bPc], [bo])
                        b.stx(rw[ct * 128:ct * 128 + n, t0:t0 + 512], o[:n, :], bo, Brw, join=True)
                for g in range(3):
                    d = DIL[g]
                    for half in range(2):
                        pa, bpa = b.psum()
                        blks = []
                        for bi8 in range(8):
                            bi = half * 8 + bi8
                            if d == 1:
                                tok = slice(bi * 128, (bi + 1) * 128)
                                blk = tt_ * 16 + bi
                            elif d == 4:
                                s_, r = bi // 4, bi % 4
                                tok = slice(s_ * 512 + r, s_ * 512 + 512, 4)
                                blk = r * 16 + tt_ * 4 + s_
                            else:
                                r = bi
                                tok = slice(r, 2048, 16)
                                blk = r * 4 + tt_
                            blks.append(blk)
                            for kc in range(8):
                                b.mm(pa[:, bi8 * 64:(bi8 + 1) * 64], hT[:, kc, tok], W1[:, kc, V_C0 + g * 64:V_C0 + (g + 1) * 64],
                                     kc == 0, kc == 7, [bW1, bhT], bpa)
                        vs_, bvs_ = vst[nvst % 2], bvst[nvst % 2]
                        nvst += 1
                        b.cp("scalar" if half == 0 else "vector", vs_[:].rearrange("p a c -> p (a c)"), pa[:, :], [bpa], [bvs_])
                        for bi8 in range(8):
                            b.stx(vv[g, blks[bi8], :, :], vs_[:, bi8, :], bvs_, Bvv, join=True)
        k.barrier()
        if upto <= 1:
            finish(b, dbo, dbg, qk=qk, rw=rw, vv=vv, Bqk=Bqk, Brw=Brw, Bvv=Bvv)
            return nc
        zt = b.sb(st, "zt", [128, 16, 2], BF16)
        bzt = Buf("zt")
        b.memset("vector", zt[:], 0.0, bzt)
        b.stx(cin[0:1024, 0:2].rearrange("(c p) t -> p c t", p=128), zt[:, 0:8, :], bzt, Bcin, join=True)
        b.stx(cinR[0:1024, 0:2].rearrange("(c p) t -> p c t", p=128), zt[:, 8:16, :], bzt, BcinR, join=True)
        zp = b.sb(st, "zp", [128, 32, TQP - TQ], BF16)
        bzp = Buf("zp")
        b.memset("vector", zp[:], 0.0, bzp)
        b.stx(cin[:, TQ:TQP].rearrange("(c p) t -> p c t", p=128), zp[:], bzp, Bcin, join=True)
        b.stx(cinR[:, TQ:TQP].rearrange("(c p) t -> p c t", p=128), zp[:], bzp, BcinR, join=True)

        with contextlib.ExitStack() as ph:
            acc = b.sb(ph, "acc", [128, S], F32)
            bacc = Buf("acc")
            qT = b.sb(ph, "qT", [64, S], BF16)
            kT = b.sb(ph, "kT", [64, S], BF16)
            bqT, bkT = Buf("qT"), Buf("kT")
            vt = b.sb(ph, "vt", [128, 64, 128], BF16)
            bvt = Buf("vt")
            bvt1 = Buf("vt1")
            b.memset("gpsimd", vt[:, :, 64:128], 1.0, bvt1)
            pT = [b.sb(ph, "pT%d" % i, [128, 256], BF16) for i in range(4)]
            bpT = [Buf("pT%d" % i) for i in range(4)]
            for g in range(3):
                d = DIL[g]
                M = S // d
                nb = M // 128
                b.ld(qT[:], qk[2 * g], bqT, src=Bqk)
                b.ld(kT[:], qk[2 * g + 1], bkT, src=Bqk)
                b.ld(vt[:, :, 0:64], vv[g].rearrange("b p c -> p b c"), bvt, src=Bvv)
                for r in range(d):
                    po, bpo = None, None
                    for n in range(nb):
                        blk = r * nb + n
                        nq = 256 if n < nb - 1 else 128
                        base = r + d * 128 * n
                        kap = kT[:, base:base + d * 127 + 1:d]
                        qap = qT[:, base:base + d * (nq - 1) + 1:d]
                        ps, bps = b.psum()
                        b.mm(ps[:, 0:nq], kap, qap, True, True, [bkT, bqT], bps)
                        p, bp = pT[blk % 4], bpT[blk % 4]
                        b.act(p[:, 0:nq], ps[:, 0:nq], AF.Exp, reads=[bps], writes=[bp], scale=0.125)
                        b.tt("gpsimd" if blk % 2 == 0 else "vector", p[:, 0:nq], p[:, 0:nq],
                             mskb[:, 0:2, :].rearrange("p a c -> p (a c)")[:, 0:nq], ALU.mult, [bp, bmsk], [bp])
                        if n % 4 == 0:
                            po, bpo = b.psum()
                        oc = (n % 4) * 128
                        if n > 0:
                            pp, bpp = pT[(blk - 1) % 4], bpT[(blk - 1) % 4]
                            b.mm(po[:, oc:oc + 128], vt[:, blk - 1, :], pp[:, 128:256], True, False, [bvt, bvt1, bpp], bpo)
                            b.mm(po[:, oc:oc + 128], vt[:, blk, :], p[:, 0:128], False, True, [bvt, bvt1, bp], bpo)
                        else:
                            b.mm(po[:, oc:oc + 128], vt[:, blk, :], p[:, 0:128], True, True, [bvt, bvt1, bp], bpo)
                        if n % 4 == 3:
                            a0_ = r + d * 128 * (n - 3)
                            aap = acc[:, a0_:a0_ + d * 511 + 1:d]
                            if g == 0:
                                b.cp("vector", aap, po[:, :], [bpo], (), joins=[bacc])
                            else:
                                b.tt("vector", aap, po[:, :], aap, ALU.add, [bpo, bacc], (), joins=[bacc])
            den = b.sb(ph, "den", [64, S], F32)
            bden = Buf("den")
            b.stx(dsc, acc[64:128, :], bacc, Bdsc, q="sync", join=True)
            b.ld(den[:], dsc, bden, src=Bdsc)
            yat = b.sb(ph, "yat", [64, S], BF16)
            byp = [Buf("yat%d" % i) for i in range(16)]
            bdp = [Buf("den%d" % i) for i in range(16)]
            wbs = b.sb(ph, "wbs", [64, 1024], F32)
            wbb = b.sb(ph, "wbb", [64, 1024], BF16)
            bwbs, bwbb = Buf("wbs"), Buf("wbb")
            b.ld(wbs[:], wba, bwbs)
            b.cp("vector", wbb[:], wbs[:], [bwbs], [bwbb])
            ast = [b.sb(ph, "ast%d" % i, [128, 8, 512], BF16) for i in range(2)]
            bast = [Buf("ast%d" % i) for i in range(2)]

            def recip_piece(sub):
                sl = slice(sub * 512, (sub + 1) * 512)
                b.k.op("vector", lambda e: e.reciprocal(out=den[:, sl], in_=den[:, sl]), reads=[bden], writes=[bdp[sub]])
                b.tt("gpsimd" if sub % 2 == 0 else "vector", yat[:, sl], acc[0:64, sl], den[:, sl], ALU.mult, [bacc, bdp[sub]], [byp[sub]])

            recip_piece(0)
            for sub in range(16):
                if sub + 1 < 16:
                    recip_piece(sub + 1)
                a_, ba_ = ast[sub % 2], bast[sub % 2]
                for ct in range(8):
                    ps, bps = b.psum()
                    b.mm(ps[:, :], wbb[:, ct * 128:(ct + 1) * 128], yat[:, sub * 512:(sub + 1) * 512], True, True, [bwbb, byp[sub]], bps)
                    eng = ("scalar", "vector")[ct % 2]
                    if ct == 0:
                        b.cp(eng, a_[:, ct, :], ps[:, :], [bps], [ba_])
                    else:
                        b.cp(eng, a_[:, ct, :], ps[:, :], [bps], (), joins=[ba_])
                store_branch(b, cin, Bcin, a_, ba_, sub, 0)
        k.barrier()
        if upto <= 2:
            finish(b, dbo, dbg, cin=cin, Bcin=Bcin)
            return nc
        RG = [[0, 1, 2, 3], [4, 5, 6, 7]]
        if not SKIP_CC:
            k.raw("gpsimd", lambda e: e.collective_compute("ReduceScatter", ALU.add, replica_groups=RG, ins=[cin], outs=[cout]),
                  reads=[Bcin], writes=[Bcout], sem=ccsem)

        with contextlib.ExitStack() as ph:
            phase3(b, ph, dbo, rw, Brw, vec, wd, wa, wg, wbr, cinR, BcinR, mskb, bmsk)
        k.barrier()
        if upto <= 3:
            finish(b, dbo, dbg)
            return nc
        k.raw("gpsimd", lambda e: e.collective_compute("ReduceScatter", ALU.add, replica_groups=RG, ins=[cinR], outs=[coutR]),
              reads=[BcinR], writes=[BcoutR], sem=ccsemR)
        x1s = b.dscr("x1s", [1024, TQP], F32)
        h2s = b.dscr("h2s", [1024, TQP], BF16)
        Bx1s, Bh2s = Buf("x1s"), Buf("h2s")
        NTL = TQ // NT4
        with contextlib.ExitStack() as ph:
            sb = lambda name, shape, dt: b.sb(ph, name, shape, dt)
            wst4 = [sb("wst4%d" % i, [128, 2048], F32) for i in range(3)]
            bwst4 = [Buf("wst4%d" % i) for i in range(3)]
            Wg = sb("Wg", [128, 8, 2048], BF16)
            Wo = sb("Wo", [128, 8, 1024], BF16)
            bWg, bWo = Buf("Wg"), Buf("Wo")
            for kc in range(16):
                w_, bw_ = wst4[kc % 3], bwst4[kc % 3]
                if kc < 8:
                    b.ld(w_[:, :], wgate[kc * 128:(kc + 1) * 128, :], bw_)
                    if kc % 2 == 0:
                        b.ts("vector", Wg[:, kc, :], w_[:, :], gvs[:, kc:kc + 1], None, ALU.mult, None, [bw_, bgvs], (), joins=[bWg])
                    else:
                        b.act(Wg[:, kc, :], w_[:, :], AF.Copy, [bw_, bgvs], (), scale=gvs[:, kc:kc + 1], joins=[bWg])
                else:
                    b.ld(w_[:, 0:1024], wout[(kc - 8) * 128:(kc - 7) * 128, :], bw_)
                    b.cp("vector" if kc % 2 == 0 else "scalar", Wo[:, kc - 8, :], w_[:, 0:1024], [bw_], (), joins=[bWo])
            bgs = sb("bgs", [128, 16], F32)
            bbgs = Buf("bgs")
            b.ld(bgs[:], bgate, bbgs)
            xt = sb("xt4", [128, 8, NT4], F32)
            sq = sb("sq4", [128, 8, NT4], BF16)
            rstd = sb("rstd4", [128, NT4], F32)
            hb = sb("hb4", [128, 8, NT4], BF16)
            gt = sb("gt4", [128, 16, NT4], BF16)
            AR = sb("AR4", [128, 16, NT4], BF16)
            m1 = sb("m14", [128, 8, NT4], BF16)
            m2 = sb("m24", [128, 8, NT4], BF16)
            mg = sb("mg4", [128, 8, NT4], BF16)
            x1 = sb("x14", [128, 8, NT4], F32)
            h2 = sb("h24", [128, 8, NT4], BF16)
            bxt, bsq, brstd, bhb, bgt, bAR, bm1, bm2, bmg, bx1, bh2 = [Buf(n) for n in "xt sq rstd hb gt AR m1 m2 mg x1 h2".split()]
            gts = b.dscr("gts", [2048, TQP], BF16)
            Bgts = Buf("gts")
            for tl in range(NTL):
                c0 = tl * NT4
                csl = slice(c0, c0 + NT4)
                b.ld(xt[:], xo.rearrange("(k p) t -> p k t", p=128)[:, :, csl], bxt)
                rmsnorm_sub(b, xt[:], bxt, NT4, sq, bsq, onesm, bones, rstd, brstd, 1e-6, lambda kc: hb[:, kc, :], bhb, True)
                for ct in range(16):
                    ps, bps = b.psum()
                    for kc in range(8):
                        b.mm(ps[:, :NT4], Wg[:, kc, ct * 128:(ct + 1) * 128], hb[:, kc, :], kc == 0, kc == 7, [bWg, bhb], bps)
                    b.act(gt[:, ct, :], ps[:, :NT4], AF.Sigmoid, [bps, bbgs], [bgt] if ct == 0 else (), bias=bgs[:, ct:ct + 1],
                          joins=() if ct == 0 else [bgt])
                b.stx(gts.rearrange("(c p) t -> p c t", p=128)[:, :, csl], gt[:], bgt, Bgts, join=True)
            for tl in range(NTL):
                c0 = tl * NT4
                csl = slice(c0, c0 + NT4)
                b.ld(xt[:], xo.rearrange("(k p) t -> p k t", p=128)[:, :, csl], bxt)
                b.ld(gt[:], gts.rearrange("(c p) t -> p c t", p=128)[:, :, csl], bgt, src=Bgts)
                b.ld(AR[:, 0:8, :], cout.rearrange("(c p) t -> p c t", p=128)[:, :, csl], bAR, src=Bcout)
                b.ld(AR[:, 8:16, :], coutR.rearrange("(c p) t -> p c t", p=128)[:, :, csl], bAR, src=BcoutR, join=True)
                b.tt("vector", m1[:], gt[:, 0:8, :], AR[:, 0:8, :], ALU.mult, [bgt, bAR], [bm1])
                b.tt("gpsimd", m2[:], gt[:, 8:16, :], AR[:, 8:16, :], ALU.mult, [bgt, bAR], [bm2])
                b.tt("vector", mg[:], m1[:], m2[:], ALU.add, [bm1, bm2], [bmg])
                for ct in range(8):
                    ps, bps = b.psum()
                    for kc in range(8):
                        b.mm(ps[:, :NT4], Wo[:, kc, ct * 128:(ct + 1) * 128], mg[:, kc, :], kc == 0, kc == 7, [bWo, bmg], bps)
                    b.tt("vector", x1[:, ct, :], ps[:, :NT4], xt[:, ct, :], ALU.add, [bps, bxt], [bx1] if ct == 0 else (),
                         joins=() if ct == 0 else [bx1])
                rmsnorm_sub(b, x1[:], bx1, NT4, sq, bsq, onesm, bones, rstd, brstd, 1e-6, lambda kc: h2[:, kc, :], bh2, True)
                b.stx(x1s.rearrange("(k p) t -> p k t", p=128)[:, :, csl], x1[:], bx1, Bx1s, join=True)
                b.stx(h2s.rearrange("(k p) t -> p k t", p=128)[:, :, csl], h2[:], bh2, Bh2s, join=True)
        k.barrier()
        with contextlib.ExitStack() as ph:
            sb = lambda name, shape, dt: b.sb(ph, name, shape, dt)
            HW = DFF // 2
            wsts = [sb("wst5%d" % i, [128, HW], F32) for i in range(2)]
            bwsts = [Buf("wst5%d" % i) for i in range(2)]
            Wu = sb("Wu", [128, 8, 2 * DFF], BF16)
            Wd = sb("Wd", [128, 22, 1024], BF16)
            bWu, bWd = Buf("Wu"), Buf("Wd")
            nst = 0
            for q4 in range(4):
                for kc in range(8):
                    w_, bw_ = wsts[nst % 2], bwsts[nst % 2]
                    b.ld(w_[:, :], wup[kc * 128:(kc + 1) * 128, q4 * HW:(q4 + 1) * HW], bw_)
                    if nst % 2 == 0:
                        b.ts("vector", Wu[:, kc, q4 * HW:(q4 + 1) * HW], w_[:, :], gvs[:, 8 + kc:9 + kc], None, ALU.mult, None,
                             [bw_, bgvs], (), joins=[bWu])
                    else:
                        b.act(Wu[:, kc, q4 * HW:(q4 + 1) * HW], w_[:, :], AF.Copy, [bw_, bgvs], (), scale=gvs[:, 8 + kc:9 + kc], joins=[bWu])
                    nst += 1
            for kc in range(22):
                w_, bw_ = wsts[nst % 2], bwsts[nst % 2]
                b.ld(w_[:, 0:1024], wdn[kc * 128:(kc + 1) * 128, :], bw_)
                b.cp("vector" if nst % 2 == 0 else "scalar", Wd[:, kc, :], w_[:, 0:1024], [bw_], (), joins=[bWd])
                nst += 1
            cws = sb("cws", [128, 44, 3], F32)
            cbs = sb("cbs", [128, 44], F32)
            bcws = Buf("cws")
            b.ld(cws[:], cw, bcws)
            b.ld(cbs[:], cb, bcws, join=True)
            crr = sb("crr", [128, 44, 2], F32)
            bcrr = Buf("crr")
            b.memset("vector", crr[:], 0.0, bcrr)
            h2 = sb("h25", [128, 8, NT4], BF16)
            x1 = sb("x15", [128, 8, NT4], F32)
            act = sb("act5", [128, 22, NT4], BF16)
            ub = [sb("ub%d" % i, [128, NT4 + 2], F32) for i in range(4)]
            yc = [sb("yc%d" % i, [128, NT4], F32) for i in range(4)]
            sg = [sb("sg%d" % i, [128, NT4], F32) for i in range(2)]
            rstd = sb("rstd5", [128, NT4], F32)
            bh2, bx1, bact, brstd = Buf("h2"), Buf("x1"), Buf("act"), Buf("rstd")
            bub = [Buf() for i in range(4)]
            byc = [Buf() for i in range(4)]
            bsg = [Buf() for i in range(2)]
            nu = 0
            for tl in range(NTL):
                c0 = tl * NT4
                csl = slice(c0, c0 + NT4)
                b.ld(h2[:], h2s.rearrange("(k p) t -> p k t", p=128)[:, :, csl], bh2, src=Bh2s)
                b.ld(x1[:], x1s.rearrange("(k p) t -> p k t", p=128)[:, :, csl], bx1, src=Bx1s)
                for i in range(22):
                    ys = []
                    for which in range(2):
                        ct = i + 22 * which
                        u_, bu_ = ub[nu % 4], bub[nu % 4]
                        y_, by_ = yc[nu % 4], byc[nu % 4]
                        nu += 1
                        ps, bps = b.psum()
                        for kc in range(8):
                            b.mm(ps[:, :NT4], Wu[:, kc, ct * 128:(ct + 1) * 128], h2[:, kc, :], kc == 0, kc == 7, [bWu, bh2], bps)
                        b.cp("gpsimd", u_[:, 0:2], crr[:, ct, :], [bcrr], [bu_])
                        b.cp("scalar", u_[:, 2:NT4 + 2], ps[:, :NT4], [bps], (), joins=[bu_])
                        b.cp("gpsimd", crr[:, ct, :], u_[:, NT4:NT4 + 2], [bu_], [bcrr])
                        b.ts("gpsimd", y_[:], u_[:, 2:NT4 + 2], cws[:, ct, 2:3], cbs[:, ct:ct + 1], ALU.mult, ALU.add, [bu_, bcws], [by_])
                        b.stt(y_[:], u_[:, 1:NT4 + 1], cws[:, ct, 1:2], y_[:], ALU.mult, ALU.add, [bu_, bcws, by_], [by_])
                        b.stt(y_[:], u_[:, 0:NT4], cws[:, ct, 0:1], y_[:], ALU.mult, ALU.add, [bu_, bcws, by_], [by_])
                        ys.append((y_, by_))
                    s_, bs_ = sg[i % 2], bsg[i % 2]
                    b.act(s_[:], ys[0][0][:], AF.Silu, [ys[0][1]], [bs_])
                    b.tt("gpsimd", act[:, i, :], s_[:], ys[1][0][:], ALU.mult, [bs_, ys[1][1]], [bact] if i == 0 else (),
                         joins=() if i == 0 else [bact])
                for ct in range(8):
                    ps, bps = b.psum()
                    for i in range(22):
                        b.mm(ps[:, :NT4], Wd[:, i, ct * 128:(ct + 1) * 128], act[:, i, :], i == 0, i == 21, [bWd, bact], bps)
                    b.tt("vector", x1[:, ct, :], ps[:, :NT4], x1[:, ct, :], ALU.add, [bps, bx1], [bx1])
                b.tt("gpsimd", act[:, 0:8, :], x1[:], x1[:], ALU.mult, [bx1], [bact])
                ps, bps = b.psum()
                for kc in range(8):
                    b.mm(ps[:, :NT4], onesm[:], act[:, kc, :], kc == 0, kc == 7, [bact, bones], bps)
                b.act(rstd[:], ps[:, :NT4], AF.Ln, [bps, b.beps], [brstd], bias=b.epst[:, 0:1])
                b.act(rstd[:], rstd[:], AF.Exp, [brstd], [brstd], scale=-0.5)
                for kc in range(8):
                    b.stt(x1[:, kc, :], x1[:, kc, :], gvs[:, 16 + kc:17 + kc], rstd[:], ALU.mult, ALU.mult, [bx1, bgvs, brstd], [bx1])
                lo = 2 if tl == 0 else 0
                b.stx(oT.rearrange("(k p) t -> p k t", p=128)[:, :, c0 + lo - 2:c0 + NT4 - 2], x1[:, :, lo:NT4], bx1, BoT, join=True)
        k.barrier(retire=False)
        k.emit()
        return nc


def store_branch(b, cin, Bcin, a_, ba_, sub, rowoff):
    q = sub // 4
    col0 = 2 + (sub % 4) * 512
    r0 = q * 1024
    b.stx(cin[r0:r0 + 1024, col0:col0 + 512].rearrange("(c p) t -> p c t", p=128), a_[:, :, :], ba_, Bcin, join=True)
    if sub % 4 == 3 and q < 3:
        r1 = (q + 1) * 1024
        b.stx(cin[r1:r1 + 1024, 0:2].rearrange("(c p) t -> p c t", p=128), a_[:, :, 510:512], ba_, Bcin, join=True)


def finish(b, dbo, dbg, **kw):
    k = b.k
    outs = []
    for name, shape, dt in dbg:
        if name not in kw:
            continue
        src = kw[name]
        Bsrc = kw["B" + name]
        bo = Buf("dbg_" + name)
        k.dma("sync", dbo[name], src, Bsrc, bo, bo)
        outs.append(bo)
    for e in ("sync",):
        waits = k._collect(e, outs, (), ())
        k.ops[e].append((waits, None, None))
    k.emit()


def _consts():
    pos = np.arange(S, dtype=np.float32)
    inv = np.power(np.float32(500000.0), -np.arange(0, 16, 2, dtype=np.float32) / np.float32(16)).astype(np.float32)
    ang = pos[None, :] * inv[:, None]
    cs = np.ones((128, S), np.float32)
    sn = np.zeros((128, S), np.float32)
    for o in (0, 64):
        cs[o + 0:o + 8] = np.cos(ang)
        cs[o + 8:o + 16] = np.cos(ang)
        sn[o + 0:o + 8] = -np.sin(ang)
        sn[o + 8:o + 16] = np.sin(ang)
    i = np.arange(128)[:, None]
    j = np.arange(128)[None, :]
    msk = np.zeros((128, 5, 128), np.float32)
    msk[:, 0] = (i <= j)
    msk[:, 1] = (i >= j)
    msk[:, 2] = (i < j)
    msk[:, 3] = (i > j)
    msk[:, 4] = (i == j)
    return cs, sn, msk


def prep_inputs(inp):
    f = lambda a: np.ascontiguousarray(np.asarray(a, dtype=np.float32))
    x = f(inp["x"])
    w_in = f(inp["w_in"])[0]
    cs, sn, msk = _consts()
    perm = np.concatenate([np.arange(8, 16), np.arange(0, 8), np.arange(16, 64)])
    RB = 2304
    colvec = lambda v: f(v).reshape(-1, 128).T
    gv = np.concatenate([colvec(inp["norm_mix_g"][0]), colvec(inp["norm_ffn_g"][0]), colvec(inp["norm_final_g"])], 1)
    mu_all = f(inp["mu_shift"])[0]
    common = {
        "gv": f(gv), "cst": cs, "snt": sn, "msk": msk,
        "wgate": f(w_in[:, RB + 3360:]), "bgate": colvec(inp["b_gate"][0]),
        "wout": f(inp["w_out"])[0], "wup": f(inp["w_ffn_up"])[0],
        "cw": f(np.asarray(inp["conv_w"])[0, :, 0, :].T.reshape(44, 128, 3).transpose(1, 0, 2)),
        "cb": colvec(inp["conv_b"][0]), "wdn": f(inp["w_ffn_down"])[0],
    }
    maps = []
    for c in range(8):
        bb, j = c // 4, c % 4
        cols = []
        qcs = [np.arange((g * 4 + j) * 64, (g * 4 + j) * 64 + 64) for g in range(3)]
        kcs = [768 + qc for qc in qcs]
        cols += [qcs[0], qcs[1], qcs[0][perm], qcs[1][perm], kcs[0], kcs[1], kcs[0][perm], kcs[1][perm],
                 qcs[2], qcs[2][perm], kcs[2], kcs[2][perm]]
        for g in range(3):
            hq = (g * 4 + j) * 64
            cols.append(1536 + np.arange(hq, hq + 64))
        rsel = []
        for part in range(3):
            rsel.append(RB + part * 1024 + j * 256 + np.arange(256))
        rsel.append(RB + 3072 + np.arange(288))
        cols += rsel
        cols = np.concatenate(cols)
        assert cols.size == NW1
        mu_sel = mu_all[np.concatenate(rsel) - RB]
        mu_t = np.zeros((128, 10), np.float32)
        for ct, (co, n) in enumerate(RW_TILES):
            mu_t[:n, ct] = mu_sel[co:co + n]
        ch = slice(j * 256, (j + 1) * 256)
        pv = lambda v: f(v).reshape(-1)[ch].reshape(2, 128).T
        k_a = pv(inp["k_a"][0])
        vecs = np.stack([pv(inp["w0"][0]), pv(inp["a0"][0]), pv(inp["k_k"][0]), k_a, pv(inp["r_k"][0]),
                         pv(inp["ln_x_w"][0]), pv(inp["ln_x_b"][0]), np.zeros_like(k_a)], axis=2)
        xTb = np.ascontiguousarray(x[bb].T)
        xo = np.zeros((1024, TQP), np.float32)
        lo = j * 2048 - 2
        if lo < 0:
            xo[:, 2:TQ] = xTb[:, 0:2048]
        else:
            xo[:, 0:TQ] = xTb[:, lo:lo + TQ]
        m = dict(common)
        m.update({
            "xT": xTb, "xo": xo, "w1": f(w_in[:, cols]), "mu": mu_t, "vec": f(vecs),
            "wd": f(inp["w_decay_up"][0][:, ch]), "wa": f(inp["w_a_up"][0][:, ch]), "wg": f(inp["w_g_up"][0][:, ch]),
            "wba": f(inp["w_branch_attn"][0][j * 64:(j + 1) * 64, :]),
            "wbr": f(inp["w_branch_rwkv"][0][ch, :]),
        })
        maps.append(m)
    return maps


_NC = None


def kernel(**inputs):
    global _NC
    if _NC is None:
        _NC = build()
    maps = prep_inputs(inputs)
    res = run_bass_kernel_spmd(_NC, maps, core_ids=list(range(8)))
    out = np.zeros((2, S, 1024), np.float32)
    for c in range(8):
        bb, j = c // 4, c % 4
        out[bb, j * 2048:(j + 1) * 2048, :] = res.results[c]["oT"].T
    return out


SKIP_CC = False
NTS = 1024
NCH = NTS // 128
C0 = 0.6065306597126334


def phase3(b, ph, dbo, rw, Brw, vec, wd, wa, wg, wbr, cin, Bcin, mskb, bmsk):
    k = b.k
    sb = lambda name, shape, dt: b.sb(ph, name, shape, dt)
    ident = mskb[:, 4, :]
    vecs = sb("vecs", [128, 2, 8], F32)
    bvecs = Buf("vecs")
    b.ld(vecs[:], vec, bvecs)
    b.ts("vector", vecs[:, :, 7], vecs[:, :, 3], -1.0, 1.0, ALU.mult, ALU.add, [bvecs], [bvecs])
    wstg = sb("wstg", [128, 1024], F32)
    bwstg = Buf("wstg")
    wdb = sb("wdb", [64, 256], BF16)
    wab = sb("wab", [64, 256], BF16)
    wgb = sb("wgb", [128, 2, 256], BF16)
    wbrb = sb("wbrb", [128, 2, 1024], BF16)
    bwdb, bwab, bwgb, bwbrb = Buf("wdb"), Buf("wab"), Buf("wgb"), Buf("wbrb")
    b.ld(wstg[0:64, 0:256], wd, bwstg)
    b.cp("vector", wdb[:], wstg[0:64, 0:256], [bwstg], [bwdb])
    b.ld(wstg[0:64, 0:256], wa, bwstg)
    b.cp("vector", wab[:], wstg[0:64, 0:256], [bwstg], [bwab])
    b.ld(wstg[:, 0:256], wg[0:128, :], bwstg)
    b.cp("vector", wgb[:, 0, :], wstg[:, 0:256], [bwstg], [bwgb])
    b.ld(wstg[0:32, 0:256], wg[128:160, :], bwstg)
    b.cp("vector", wgb[0:32, 1, :], wstg[0:32, 0:256], [bwstg], (), joins=[bwgb])
    for kc in range(2):
        b.ld(wstg[:, :], wbr[kc * 128:(kc + 1) * 128, :], bwstg)
        b.cp("vector", wbrb[:, kc, :], wstg[:, :], [bwstg], (), joins=[bwbrb])
    bd = sb("bd", [128, 128], BF16)
    bbd = Buf("bd")
    b.memset("vector", bd[:], 0.0, bbd)
    b.memset("vector", bd[0:64, 0:64], 1.0, bbd)
    b.memset("vector", bd[64:128, 64:128], 1.0, bbd)
    mk2 = sb("mk2", [128, 256], BF16)
    bmk2 = Buf("mk2")
    b.cp("vector", mk2[:, 0:128], mskb[:, 2, :], [bmsk], [bmk2])
    b.cp("vector", mk2[:, 128:256], mskb[:, 0, :], [bmsk], (), joins=[bmk2])
    rmask = sb("rmask", [128, NTS], F32)
    brmask = Buf("rmask")
    b.memset("vector", rmask[:], 1.0, brmask)
    b.memset("vector", rmask[:, 0:NTS:128], 0.0, brmask)
    yrw = sb("yrw", [128, 2, S], BF16)
    byrw = Buf("yrw")
    names = "R Kb Vb Ab CS KK Em Ev TA TB".split()
    A = {n: sb(n, [128, NTS], F32) for n in names}
    Bf = {n: Buf(n) for n in names}
    zg = sb("zg", [128, 2, NTS], F32)
    bzg = Buf("zg")
    tw = sb("tw", [64, NTS], BF16)
    zab = sb("zab", [64, NTS], BF16)
    sgb = sb("sgb", [128, 2, NTS], BF16)
    sqb = sb("sqb", [128, NTS], BF16)
    btw, bzab, bsgb, bsqb = [Buf(n) for n in "tw zab sgb sqb".split()]

    class Cx:
        pass
    cxs = []
    for i in range(2):
        cx = Cx()
        cx.ar = sb("ar%d" % i, [128, NCH, 256], BF16)
        cx.kt = sb("kt%d" % i, [128, NTS], BF16)
        cx.bt = sb("bt%d" % i, [128, NTS], BF16)
        cx.vb = sb("vb%d" % i, [128, NTS], BF16)
        cx.Ep = sb("Ep%d" % i, [128, NTS], F32)
        cx.YF = sb("YF%d" % i, [128, NTS], F32)
        cx.BS = sb("BS%d" % i, [128, NTS], F32)
        cx.G = sb("G%d" % i, [128, NTS], F32)
        cx.bar, cx.bkt, cx.bbt, cx.bvb, cx.bEp, cx.bYF, cx.bBS, cx.bG = [Buf() for _ in range(8)]
        cxs.append(cx)
    Sst = sb("Sst", [128, 64], F32)
    Sbf = sb("Sbf", [128, 64], BF16)
    tmpS = sb("tmpS", [128, 64], F32)
    bSst, bSbf, btmpS = Buf("Sst"), Buf("Sbf"), Buf("tmpS")
    NSL = 4
    tok = [sb("tok%d" % i, [128, 3, 128], BF16) for i in range(NSL)]
    btok = [Buf("tok%d" % i) for i in range(NSL)]
    LM2 = [sb("LM%d" % i, [128, 2, 256], BF16) for i in range(NSL)]
    LK2 = [sb("LK%d" % i, [128, 2, 256], BF16) for i in range(NSL)]
    Xb2 = [sb("Xb%d" % i, [128, 256], BF16) for i in range(NSL)]
    bLM2 = [Buf() for i in range(NSL)]
    bLK2 = [Buf() for i in range(NSL)]
    bXb2 = [Buf() for i in range(NSL)]
    AA2 = [[sb("AA%d%d" % (sl, i), [128, 512], BF16) for i in range(2)] for sl in range(NSL)]
    bAA2 = [[Buf() for i in range(2)] for sl in range(NSL)]
    Xs2 = sb("Xs2", [128, 128], BF16)
    Us2 = sb("Us2", [128, 128], BF16)
    bXs2, bUs2 = Buf("Xs2"), Buf("Us2")
    mkx = sb("mkx", [128, 512], BF16)
    mlx = sb("mlx", [128, 256], BF16)
    idx2 = sb("idx2", [128, 256], BF16)
    bmkx = Buf("mkx")
    b.cp("vector", mkx[:, 0:128], mskb[:, 2, :], [bmsk], [bmkx])
    b.cp("vector", mkx[:, 128:256], mskb[:, 2, :], [bmsk], (), joins=[bmkx])
    b.cp("vector", mkx[:, 256:384], mskb[:, 0, :], [bmsk], (), joins=[bmkx])
    b.cp("vector", mkx[:, 384:512], mskb[:, 0, :], [bmsk], (), joins=[bmkx])
    for h in range(2):
        b.cp("vector", mlx[:, h * 128:(h + 1) * 128], mskb[:, 3, :], [bmsk], (), joins=[bmkx])
        b.cp("vector", idx2[:, h * 128:(h + 1) * 128], mskb[:, 4, :], [bmsk], (), joins=[bmkx])
    ysb = sb("ysb", [128, 128], F32)
    bysb = Buf("ysb")
    stt6 = sb("stt6", [128, 2, 6], F32)
    mv = sb("mv", [128, 2, 2], F32)
    rs2 = sb("rs2", [128, 2], F32)
    yn = sb("yn", [128, 128], BF16)
    bst6, bmv, brs2, byn = Buf("st6"), Buf("mv"), Buf("rs2"), Buf("yn")

    def pieces():
        for i in range(NTS // 512):
            yield slice(i * 512, (i + 1) * 512)

    def prep(cx, p, stile):
        pc = slice(p * 128, (p + 1) * 128)
        t0 = stile * NTS
        tsl = slice(t0, t0 + NTS)
        cx.p = p
        cx.tsl = tsl
        ar, kt, bt, vb = cx.ar, cx.kt, cx.bt, cx.vb
        b.ld(A["R"][:], rw[p * 128:(p + 1) * 128, tsl], Bf["R"], src=Brw)
        b.ld(A["Kb"][:], rw[(2 + p) * 128:(3 + p) * 128, tsl], Bf["Kb"], src=Brw)
        b.ld(A["Vb"][:], rw[(4 + p) * 128:(5 + p) * 128, tsl], Bf["Vb"], src=Brw)
        b.ld(A["TA"][0:64, :], rw[768:832, tsl], Bf["TA"], src=Brw)
        b.ld(A["TB"][0:64, :], rw[896:960, tsl], Bf["TB"], src=Brw)
        b.ld(zg[:, 0, :], rw[1024:1152, tsl], bzg, src=Brw)
        b.ld(zg[0:32, 1, :], rw[1152:1184, tsl], bzg, src=Brw, join=True)
        yield
        b.act(tw[:], A["TA"][0:64, :], AF.Tanh, [Bf["TA"]], [btw])
        b.cp("gpsimd", zab[:], A["TB"][0:64, :], [Bf["TB"]], [bzab])
        b.act(sgb[:, 0, :], zg[:, 0, :], AF.Sigmoid, [bzg], [bsgb])
        b.act(sgb[0:32, 1, :], zg[0:32, 1, :], AF.Sigmoid, [bzg], (), joins=[bsgb])
        yield
        first = True
        for sl in pieces():
            ps, bps = b.psum()
            b.mm(ps[:, :], wdb[:, pc], tw[:, sl], True, True, [bwdb, btw], bps)
            b.act(A["TA"][:, sl], ps[:, :], AF.Sigmoid, [bps, bvecs], [Bf["TA"]] if first else (), bias=vecs[:, p, 0:1],
                  joins=() if first else [Bf["TA"]])
            ps, bps = b.psum()
            b.mm(ps[:, :], wab[:, pc], zab[:, sl], True, True, [bwab, bzab], bps)
            b.act(A["Ab"][:, sl], ps[:, :], AF.Sigmoid, [bps, bvecs], [Bf["Ab"]] if first else (), bias=vecs[:, p, 1:2],
                  joins=() if first else [Bf["Ab"]])
            ps, bps = b.psum()
            b.mm(ps[:, :], wgb[:, 0, pc], sgb[:, 0, sl], True, False, [bwgb, bsgb], bps)
            b.mm(ps[:, :], wgb[0:32, 1, pc], sgb[0:32, 1, sl], False, True, [bwgb, bsgb], bps)
            b.cp("vector", cx.G[:, sl], ps[:, :], [bps], [cx.bG] if first else (), joins=() if first else [cx.bG])
            first = False
            yield
        b.k.op("vector", lambda e: e.tensor_tensor_scan(out=A["CS"][:], data0=rmask[:], data1=A["TA"][:], initial=0.0,
                                                        op0=ALU.mult, op1=ALU.add),
               reads=[brmask, Bf["TA"]], writes=[Bf["CS"]])
        yield
        b.act(cx.Ep[:], A["CS"][:], AF.Exp, [Bf["CS"]], [cx.bEp], scale=-C0)
        b.act(A["Em"][:], A["CS"][:], AF.Exp, [Bf["CS"]], [Bf["Em"]], scale=C0)
        b.tt("gpsimd", A["TB"][:], A["CS"][:], A["TA"][:], ALU.subtract, [Bf["CS"], Bf["TA"]], [Bf["TB"]])
        b.act(A["Ev"][:], A["TB"][:], AF.Exp, [Bf["TB"]], [Bf["Ev"]], scale=-C0)
        yield
        b.act(sqb[:], A["Kb"][:], AF.Square, [Bf["Kb"], bvecs], [bsqb], scale=vecs[:, p, 2:3])
        first = True
        for sl in pieces():
            ps, bps = b.psum()
            b.mm(ps[:, :], bd[:], sqb[:, sl], True, True, [bbd, bsqb], bps)
            b.act(A["TA"][:, sl], ps[:, :], AF.Ln, [bps, b.beps], [Bf["TA"]] if first else (), bias=b.epst[:, 2:3],
                  joins=() if first else [Bf["TA"]])
            first = False
        yield
        b.act(A["TA"][:], A["TA"][:], AF.Exp, [Bf["TA"]], [Bf["TA"]], scale=-0.5)
        b.stt(A["KK"][:], A["Kb"][:], vecs[:, p, 2:3], A["TA"][:], ALU.mult, ALU.mult, [Bf["Kb"], bvecs, Bf["TA"]], [Bf["KK"]])
        yield
        b.ts("gpsimd", A["TB"][:], A["Ab"][:], vecs[:, p, 3:4], vecs[:, p, 7:8], ALU.mult, ALU.add, [Bf["Ab"], bvecs], [Bf["TB"]])
        b.tt("gpsimd", A["Kb"][:], A["Kb"][:], A["TB"][:], ALU.mult, [Bf["Kb"], Bf["TB"]], [Bf["Kb"]])
        yield
        b.tt("vector", A["TB"][:], A["KK"][:], A["Ab"][:], ALU.mult, [Bf["KK"], Bf["Ab"]], [Bf["TB"]])
        b.tt("vector", bt[:], A["TB"][:], A["Em"][:], ALU.mult, [Bf["TB"], Bf["Em"]], [cx.bbt])
        b.stt(ar[:, :, 0:128], A["KK"][:].rearrange("p (c t) -> p c t", t=128), -1.0,
              A["Ev"][:].rearrange("p (c t) -> p c t", t=128), ALU.mult, ALU.mult, [Bf["KK"], Bf["Ev"]], [cx.bar])
        b.tt("gpsimd", ar[:, :, 128:256], A["R"][:].rearrange("p (c t) -> p c t", t=128),
             cx.Ep[:].rearrange("p (c t) -> p c t", t=128), ALU.mult, [Bf["R"], cx.bEp], (), joins=[cx.bar])
        yield
        b.tt("gpsimd", kt[:], A["Kb"][:], A["Em"][:], ALU.mult, [Bf["Kb"], Bf["Em"]], [cx.bkt])
        b.cp("scalar", vb[:], A["Vb"][:], [Bf["Vb"]], [cx.bvb])
        yield
        b.tt("vector", A["TB"][:], A["R"][:], A["Kb"][:], ALU.mult, [Bf["R"], Bf["Kb"]], [Bf["TB"]])
        b.ts("vector", sqb[:], A["TB"][:], vecs[:, p, 4:5], None, ALU.mult, None, [Bf["TB"], bvecs], [bsqb])
        first = True
        for sl in pieces():
            ps, bps = b.psum()
            b.mm(ps[:, :], bd[:], sqb[:, sl], True, True, [bbd, bsqb], bps)
            b.tt("vector", cx.BS[:, sl], ps[:, :], A["Vb"][:, sl], ALU.mult, [bps, Bf["Vb"]], [cx.bBS] if first else (),
                 joins=() if first else [cx.bBS])
            first = False

    def pre(cx, c, par):
        ar, kt, bt, vb = cx.ar, cx.kt, cx.bt, cx.vb
        bar, bkt, bbt, bvb = cx.bar, cx.bkt, cx.bbt, cx.bvb
        csl = slice(c * 128, (c + 1) * 128)
        HS = [slice(0, 64), slice(64, 128)]
        ps, bps = b.psum()
        b.mm(ps[:, 0:128], vb[:, csl], ident, True, True, [bvb, bmsk], bps)
        b.mm(ps[:, 128:256], kt[:, csl], ident, True, True, [bkt, bmsk], bps)
        b.mm(ps[:, 256:384], bt[:, csl], ident, True, True, [bbt, bmsk], bps)
        b.cp("scalar", tok[par][:].rearrange("p a c -> p (a c)"), ps[:, 0:384], [bps], [btok[par]])
        yield
        for h in range(2):
            ps, bps = b.psum()
            b.mm(ps[:, 0:256], bt[HS[h], csl], ar[HS[h], c, :], True, True, [bbt, bar], bps)
            b.tt("vector", LM2[par][:, h, :], ps[:, 0:256], mk2[:], ALU.mult, [bps, bmk2], [bLM2[par]] if h == 0 else (),
                 joins=() if h == 0 else [bLM2[par]])
            ps, bps = b.psum()
            b.mm(ps[:, 0:256], kt[HS[h], csl], ar[HS[h], c, :], True, True, [bkt, bar], bps)
            b.tt("vector", LK2[par][:, h, :], ps[:, 0:256], mk2[:], ALU.mult, [bps, bmk2], [bLK2[par]] if h == 0 else (),
                 joins=() if h == 0 else [bLK2[par]])
            ps, bps = b.psum()
            b.mm(ps[:, 0:128], ar[HS[h], c, 0:128], bt[HS[h], csl], True, True, [bbt, bar], bps)
            b.tt("vector", AA2[par][0][:, 256 + h * 128:256 + (h + 1) * 128], ps[:, 0:128], mskb[:, 3, :], ALU.mult, [bps, bmsk],
                 [bAA2[par][0]] if h == 0 else (), joins=() if h == 0 else [bAA2[par][0]])
            b.tt("gpsimd", Xb2[par][:, h * 128:(h + 1) * 128], LM2[par][:, h, 0:128], ident, ALU.add, [bLM2[par], bmsk],
                 [bXb2[par]] if h == 0 else (), joins=() if h == 0 else [bXb2[par]])
        yield
        for lev in range(1, 7):
            src, bsrc = AA2[par][(lev - 1) % 2], bAA2[par][(lev - 1) % 2]
            dst, bdst = AA2[par][lev % 2], bAA2[par][lev % 2]
            ps, bps = b.psum()
            for h in range(2):
                if lev == 1:
                    A_ = LM2[par][:, h, 0:128]
                    rdA = [bsrc, bLM2[par]]
                else:
                    A_ = src[:, h * 128:(h + 1) * 128]
                    rdA = [bsrc]
                AT_ = src[:, 256 + h * 128:256 + (h + 1) * 128]
                if lev < 6:
                    b.mm(ps[:, h * 128:(h + 1) * 128], AT_, A_, True, True, rdA, bps)
                b.mm(ps[:, 256 + h * 128:256 + (h + 1) * 128], A_, AT_, True, True, rdA, bps)
            if lev < 6:
                b.cp("scalar", dst[:, :], ps[:, 0:512], [bps], [bdst])
            else:
                b.cp("scalar", dst[:, 256:512], ps[:, 256:512], [bps], [bdst])
            ps2, bps2 = b.psum()
            for h in range(2):
                b.mm(ps2[:, h * 128:(h + 1) * 128], dst[:, 256 + h * 128:256 + (h + 1) * 128], Xb2[par][:, h * 128:(h + 1) * 128], True, True,
                     [bdst, bXb2[par]], bps2)
            b.tt("vector", Xb2[par][:], ps2[:, 0:256], Xb2[par][:], ALU.add, [bps2, bXb2[par]], [bXb2[par]])
            yield

    SP = sb("SPst", [128, 64], F32)
    bSP = Buf("SP")

    def stateg(cx, c, par):
        ar, bar = cx.ar, cx.bar
        pcol = slice(c * 128 + 127, c * 128 + 128)
        psY, bpsY = b.pt[5 + (c % 2)], b.pb[5 + (c % 2)]
        psS, bpsS = b.pt[7], b.pb[7]
        HS = [slice(0, 64), slice(64, 128)]
        b.ts("gpsimd", SP[:, :], Sst[:, :], cx.Ep[:, pcol], None, ALU.mult, None, [bSst, cx.bEp], [bSP])
        for h in range(2):
            hs = HS[h]
            ps, bps = b.psum()
            b.mm(ps[:, 0:64], ar[hs, c, 0:128], Sbf[hs, :], True, False, [bar, bSbf], bps)
            b.mm(ps[:, 0:64], LK2[par][:, h, 0:128], tok[par][:, 0, hs], False, True, [bLK2[par], btok[par]], bps)
            b.cp("scalar", Xs2[:, hs], ps[:, 0:64], [bps], [bXs2] if h == 0 else (), joins=() if h == 0 else [bXs2])
        yield
        ps, bps = b.psum()
        for h in range(2):
            b.mm(ps[:, HS[h]], Xb2[par][:, h * 128:(h + 1) * 128], Xs2[:, h * 64:(h + 1) * 64], True, True, [bXb2[par], bXs2], bps)
        b.cp("scalar", Us2[:], ps[:, 0:128], [bps], [bUs2])
        yield
        for h in range(2):
            hs = HS[h]
            b.mm(psS[:, hs], tok[par][:, 2, :], Us2[:, h * 64:(h + 1) * 64], True, False, [btok[par], bUs2], bpsS)
            b.mm(psS[:, hs], tok[par][:, 1, :], tok[par][:, 0, hs], False, True, [btok[par]], bpsS)
        for h in range(2):
            hs = HS[h]
            b.mm(psY[:, hs], ar[hs, c, 128:256], Sbf[hs, :], True, False, [bar, bSbf], bpsY)
            b.mm(psY[:, hs], LM2[par][:, h, 128:256], Us2[:, h * 64:(h + 1) * 64], False, False, [bLM2[par], bUs2], bpsY)
            b.mm(psY[:, hs], LK2[par][:, h, 128:256], tok[par][:, 0, hs], False, True, [bLK2[par], btok[par]], bpsY)
        for h in range(2):
            hs = HS[h]
            b.stt(Sbf[hs, :], psS[hs, hs], cx.Ep[hs, pcol], SP[hs, :], ALU.mult, ALU.add, [bpsS, cx.bEp, bSP],
                  [bSbf] if h == 0 else (), joins=() if h == 0 else [bSbf])
        for h in range(2):
            hs = HS[h]
            b.stt(Sst[hs, :], psS[hs, hs], cx.Ep[hs, pcol], SP[hs, :], ALU.mult, ALU.add, [bpsS, cx.bEp, bSP],
                  [bSst] if h == 0 else (), joins=() if h == 0 else [bSst])
        yield

    def outg(cx, c):
        p = cx.p
        csl = slice(c * 128, (c + 1) * 128)
        psY, bpsY = b.pt[5 + (c % 2)], b.pb[5 + (c % 2)]
        for h in range(2):
            b.k.op("vector", lambda e, h=h: e.bn_stats(out=stt6[:, h, :], in_=psY[:, h * 64:(h + 1) * 64]),
                   reads=[bpsY], writes=[bst6] if h == 0 else (), joins=() if h == 0 else [bst6])
        for h in range(2):
            b.k.op("vector", lambda e, h=h: e.bn_aggr(out=mv[:, h, :], in_=stt6[:, h, :]),
                   reads=[bst6], writes=[bmv] if h == 0 else (), joins=() if h == 0 else [bmv])
        b.act(rs2[:], mv[:, :, 1], AF.Ln, [bmv, b.beps], [brs2], bias=b.epst[:, 1:2])
        b.act(rs2[:], rs2[:], AF.Exp, [brs2], [brs2], scale=-0.5)
        yield
        for h in range(2):
            b.ts("vector", yn[:, h * 64:(h + 1) * 64], psY[:, h * 64:(h + 1) * 64], mv[:, h, 0:1], rs2[:, h:h + 1],
                 ALU.subtract, ALU.mult, [bpsY, bmv, brs2], [byn] if h == 0 else (), joins=() if h == 0 else [byn])
        ps, bps = b.psum()
        b.mm(ps[:, 0:128], yn[:], ident, True, True, [byn, bmsk], bps)
        b.ts("vector", cx.YF[:, csl], ps[:, 0:128], vecs[:, p, 5:6], vecs[:, p, 6:7], ALU.mult, ALU.add, [bps, bvecs],
             [cx.bYF] if c == 0 else (), joins=() if c == 0 else [cx.bYF])
        yield

    def finalize(cx):
        b.tt("vector", cx.YF[:], cx.YF[:], cx.BS[:], ALU.add, [cx.bYF, cx.bBS], [cx.bYF])
        b.tt("gpsimd", yrw[:, cx.p, cx.tsl], cx.YF[:], cx.G[:], ALU.mult, [cx.bYF, cx.bG], (), joins=[byrw])

    def step(g):
        try:
            next(g)
            return True
        except StopIteration:
            return False

    ast = [sb("rst%d" % i, [128, 8, 512], BF16) for i in range(2)]
    bast = [Buf("rst%d" % i) for i in range(2)]

    def rproj(sub):
        a_, ba_ = ast[sub % 2], bast[sub % 2]
        for ct in range(8):
            ps, bps = b.psum()
            for kc in range(2):
                b.mm(ps[:, :], wbrb[:, kc, ct * 128:(ct + 1) * 128], yrw[:, kc, sub * 512:(sub + 1) * 512], kc == 0, kc == 1, [bwbrb, byrw], bps)
            eng = ("scalar", "vector")[ct % 2]
            if ct == 0:
                b.cp(eng, a_[:, ct, :], ps[:, :], [bps], [ba_])
            else:
                b.cp(eng, a_[:, ct, :], ps[:, :], [bps], (), joins=[ba_])
        store_branch(b, cin, Bcin, a_, ba_, sub, 1024)

    b.nring = 5
    NCHT = S // 128
    AHEAD = 3
    QUOTA = 3
    for p in range(2):
        b.memset("vector", Sst[:], 0.0, bSst)
        b.memset("gpsimd", Sbf[:], 0.0, bSbf)
        pregens = {}
        prepgens = {}
        for r in range(-AHEAD, NCHT + 1):
            gens = []
            if 0 <= r < NCHT:
                gens.append([stateg(cxs[(r // NCH) % 2], r % NCH, r % NSL), 99])
            if 1 <= r <= NCHT:
                gens.append([outg(cxs[((r - 1) // NCH) % 2], (r - 1) % NCH), 99])
            for dd in range(1, AHEAD + 1):
                gc = r + dd
                if 0 <= gc < NCHT:
                    if gc not in pregens:
                        cxn = cxs[(gc // NCH) % 2]
                        if gc % NCH == 0:
                            st_ = gc // NCH
                            if st_ not in prepgens:
                                prepgens[st_] = prep(cxn, p, st_)
                            while step(prepgens[st_]):
                                pass
                        pregens[gc] = pre(cxn, gc % NCH, gc % NSL)
                    gens.append([pregens[gc], QUOTA])
            nst_ = (r + AHEAD) // NCH + 1
            if r >= 0 and nst_ < S // NTS and 1 <= (r % NCH) <= NCH - AHEAD - 1:
                if nst_ not in prepgens:
                    prepgens[nst_] = prep(cxs[nst_ % 2], p, nst_)
                gens.append([prepgens[nst_], 3])
            alive = True
            while alive:
                alive = False
                for g_ in gens:
                    if g_[1] > 0:
                        g_[1] -= 1
                        if step(g_[0]):
                            alive = True
                        else:
                            g_[1] = 0
            if r + 1 in pregens:
                while step(pregens[r + 1]):
                    pass
                del pregens[r + 1]
            if r >= 1 and (r - 1) % NCH == NCH - 1:
                finalize(cxs[((r - 1) // NCH) % 2])
                if p == 1:
                    st_done = (r - 1) // NCH
                    for sub in range(st_done * (NTS // 512), (st_done + 1) * (NTS // 512)):
                        rproj(sub)
    b.nring = 8
    if "yrw" in dbo:
        b.stx(dbo["yrw"], yrw[:], byrw, Buf("dbgyrw"), join=True)


def xupd_sb(b, ps2, bps2, X, bX):
    b.tt("vector", X[:], ps2[:, 0:128], X[:], ALU.add, [bps2, bX], [bX])
```

```python
import contextlib
import numpy as np
import ml_dtypes
import concourse.bass as bass
import concourse.mybir as mybir
from concourse.bass_utils import run_bass_kernel_spmd

F32 = mybir.dt.float32
BF16 = mybir.dt.bfloat16
ALU = mybir.AluOpType
AF = mybir.ActivationFunctionType

COMPUTE = ("tensor", "vector", "scalar", "gpsimd")
ALLENG = ("sync", "tensor", "vector", "scalar", "gpsimd")

S = 8192
TQ = 2050
TQP = 2064
NT4 = 410
DFF = 2816


class Buf:
    __slots__ = ("name", "writers", "readers", "war", "sems")

    def __init__(self, name=""):
        self.name = name
        self.writers = {}
        self.readers = {}
        self.war = {}
        self.sems = {}


class K:
    def __init__(self, nc, stack):
        self.nc = nc
        self.stack = stack
        self.ops = {e: [] for e in ALLENG}
        self.count = {e: 0 for e in COMPUTE}
        self.esem = {e: stack.enter_context(nc.semaphore("sem_" + e)) for e in COMPUTE}
        self.waited = {e: {} for e in ALLENG}
        self.nsem = 0
        self.dbufs = []
        self.freesems = {}
        self.rawtot = {}

    def newsem(self, name):
        self.nsem += 1
        return self.stack.enter_context(self.nc.semaphore(name))

    def _collect(self, eng, reads, writes, joins):
        deps = {}

        def add(d):
            for sem, (val, en) in d.items():
                if eng == "tensor" and en == "tensor":
                    continue
                if sem not in deps or deps[sem] < val:
                    deps[sem] = val
        for b in reads:
            add(b.writers)
        for b in writes:
            add(b.writers)
            add(b.readers)
        for b in joins:
            add(b.war)
        w = self.waited[eng]
        out = []
        for sem, val in deps.items():
            if w.get(sem, 0) < val:
                w[sem] = val
                out.append((sem, val))
        return out

    def _commit(self, ev, reads, writes, joins):
        sem, val, en = ev
        for b in reads:
            b.readers[sem] = (val, en)
        for b in writes:
            war = dict(b.writers)
            for s, v in b.readers.items():
                if s not in war or war[s][0] < v[0]:
                    war[s] = v
            b.war = war
            b.writers = {sem: (val, en)}
            b.readers = {}
        for b in joins:
            b.writers[sem] = (val, en)

    def op(self, eng, fn, reads=(), writes=(), joins=()):
        waits = self._collect(eng, reads, writes, joins)
        self.count[eng] += 1
        n = self.count[eng]
        sem = self.esem[eng]
        self._commit((sem, n, eng), reads, writes, joins)
        self.ops[eng].append((waits, fn, (sem, 1)))

    def dma(self, q, out_ap, in_ap, src, dst, owner, join=False):
        reads = [src]
        writes = [] if join else [dst]
        joins = [dst] if join else []
        waits = self._collect(q, reads, writes, joins)
        kind = "sw" if q == "gpsimd" else "hw"
        if kind not in owner.sems:
            fs = self.freesems.setdefault(kind, [])
            if fs:
                owner.sems[kind] = list(fs.pop())
            else:
                owner.sems[kind] = [self.newsem("dsem_%s%d" % (kind, self.nsem)), 0]
            if owner not in self.dbufs:
                self.dbufs.append(owner)
        st_ = owner.sems[kind]
        st_[1] += 16
        ev = (st_[0], st_[1], "dma")
        self._commit(ev, reads, writes, joins)

        def fn(e, out_ap=out_ap, in_ap=in_ap):
            return e.dma_start(out=out_ap, in_=in_ap)
        self.ops[q].append((waits, fn, (st_[0], 16)))

    def raw(self, eng, fn, reads, writes, sem, inc=1):
        waits = self._collect(eng, reads, writes, ())
        tot = self.rawtot.get(sem, 0) + inc
        self.rawtot[sem] = tot
        self._commit((sem, tot, "raw"), reads, writes, ())
        self.ops[eng].append((waits, fn, (sem, inc)))

    def barrier(self, retire=True):
        allev = {}
        for e in COMPUTE:
            if self.count[e]:
                allev[self.esem[e]] = self.count[e]
        for b in self.dbufs:
            for sem_, tot_ in b.sems.values():
                allev[sem_] = max(allev.get(sem_, 0), tot_)
        for sem, tot in self.rawtot.items():
            allev[sem] = tot
        for e in ALLENG:
            w = self.waited[e]
            waits = []
            for sem, val in allev.items():
                if w.get(sem, 0) < val:
                    w[sem] = val
                    waits.append((sem, val))
            self.ops[e].append((waits, None, None))
        if retire:
            for b in self.dbufs:
                for kind, (sem_, tot_) in b.sems.items():
                    self.freesems.setdefault(kind, []).append((sem_, tot_))
                b.sems = {}
            self.dbufs = []

    def emit(self):
        nc = self.nc
        with nc.Block() as block:
            def run(e, name):
                embed_ok = name in ("vector", "scalar", "gpsimd")
                for waits, fn, inc in self.ops[name]:
                    is_compute = fn is not None and inc is not None and name in self.esem and inc[0] is self.esem[name]
                    if embed_ok and is_compute and waits:
                        for sem, val in waits[1:]:
                            e.wait_ge(sem, val)
                        ins = fn(e)
                        ins._wait_ge(waits[0][0], waits[0][1])
                        ins.then_inc(inc[0], inc[1])
                        continue
                    for sem, val in waits:
                        e.wait_ge(sem, val)
                    if fn is not None:
                        ins = fn(e)
                        if inc is not None:
                            ins.then_inc(inc[0], inc[1])

            @block.sync
            def _(e):
                run(e, "sync")

            @block.tensor
            def _(e):
                run(e, "tensor")

            @block.vector
            def _(e):
                run(e, "vector")

            @block.scalar
            def _(e):
                run(e, "scalar")

            @block.gpsimd
            def _(e):
                run(e, "gpsimd")


class B:
    def __init__(self, nc, st):
        self.nc = nc
        self.k = K(nc, st)
        self.st = st
        self.ext = Buf("ext")
        self.pt = []
        self.pb = []
        for i in range(8):
            self.pt.append(st.enter_context(nc.psum_tensor("ps%d" % i, [128, 512], F32)))
            self.pb.append(Buf("ps%d" % i))
        self.pi = 0
        self.nring = 8

    def psum(self):
        i = self.pi % self.nring
        self.pi = (i + 1) % self.nring
        return self.pt[i], self.pb[i]

    def din(self, name, shape, dt=F32):
        return self.nc.dram_tensor(name, list(shape), dt, kind="ExternalInput").ap()

    def dout(self, name, shape, dt=F32):
        return self.nc.dram_tensor(name, list(shape), dt, kind="ExternalOutput").ap()

    def dscr(self, name, shape, dt):
        return self.nc.dram_tensor(name, list(shape), dt).ap()

    def sb(self, stack, name, shape, dt):
        return stack.enter_context(self.nc.sbuf_tensor(name, list(shape), dt))

    def mm(self, out, lhsT, rhs, start, stop, reads, wb):
        self.k.op("tensor", lambda e: e.matmul(out, lhsT=lhsT, rhs=rhs, start=start, stop=stop),
                  reads=reads, writes=[wb])

    def act(self, out, in_, func, reads, writes, bias=None, scale=None, eng="scalar", joins=()):
        kw = {}
        if bias is not None:
            kw["bias"] = bias
        if scale is not None:
            kw["scale"] = scale
        self.k.op(eng, lambda e: e.activation(out=out, in_=in_, func=func, **kw), reads=reads, writes=writes, joins=joins)

    def tt(self, eng, out, in0, in1, op, reads, writes, joins=()):
        self.k.op(eng, lambda e: e.tensor_tensor(out=out, in0=in0, in1=in1, op=op), reads=reads, writes=writes, joins=joins)

    def ts(self, eng, out, in0, s1, s2, op0, op1, reads, writes, joins=()):
        if s2 is None:
            self.k.op(eng, lambda e: e.tensor_scalar(out=out, in0=in0, scalar1=s1, scalar2=None, op0=op0), reads=reads, writes=writes, joins=joins)
        else:
            self.k.op(eng, lambda e: e.tensor_scalar(out=out, in0=in0, scalar1=s1, scalar2=s2, op0=op0, op1=op1), reads=reads, writes=writes, joins=joins)

    def stt(self, out, in0, scalar, in1, op0, op1, reads, writes, joins=()):
        self.k.op("vector", lambda e: e.scalar_tensor_tensor(out=out, in0=in0, scalar=scalar, in1=in1, op0=op0, op1=op1),
                  reads=reads, writes=writes, joins=joins)

    def cp(self, eng, out, in_, reads, writes, joins=()):
        if eng == "scalar":
            self.k.op(eng, lambda e: e.copy(out=out, in_=in_), reads=reads, writes=writes, joins=joins)
        else:
            self.k.op(eng, lambda e: e.tensor_copy(out=out, in_=in_), reads=reads, writes=writes, joins=joins)

    def memset(self, eng, ap, val, wb):
        self.k.op(eng, lambda e: e.memset(ap, val), reads=(), writes=[wb])

    def ld(self, out_ap, in_ap, dst, src=None, q="sync", join=False):
        self.k.dma(q, out_ap, in_ap, src if src is not None else self.ext, dst, dst, join=join)

    def stx(self, out_ap, in_ap, src, dst, q="gpsimd", join=False):
        self.k.dma(q, out_ap, in_ap, src, dst, src, join=join)


QK_C0 = 0
V_C0 = 768
RW_C0 = 960
NW1 = 2016
RW_TILES = [(0, 128), (128, 128), (256, 128), (384, 128), (512, 128), (640, 128),
            (768, 64), (832, 64), (896, 128), (1024, 32)]
DIL = (1, 4, 16)


def rmsnorm_sub(b, xs, bxs, N, sq, bsq, onesm, bones, rstd, brstd, eps, outfn, bout, first_out):
    b.act(sq[:, :, :N], xs, AF.Square, reads=[bxs], writes=[bsq])
    ps, pbuf = b.psum()
    for kc in range(8):
        b.mm(ps[:, :N], onesm[:], sq[:, kc, :N], kc == 0, kc == 7, [bsq, bones], pbuf)
    b.act(rstd[:, :N], ps[:, :N], AF.Ln, reads=[pbuf, b.beps], writes=[brstd], bias=b.epst[:, 0:1] if eps == 1e-6 else b.epst[:, 1:2])
    b.act(rstd[:, :N], rstd[:, :N], AF.Exp, reads=[brstd], writes=[brstd], scale=-0.5)
    for kc in range(8):
        eng = "gpsimd" if kc in (1, 4, 6) else "vector"
        if kc == 0 and first_out:
            b.tt(eng, outfn(kc), xs[:, kc, :], rstd[:, :N], ALU.mult, reads=[bxs, brstd], writes=[bout])
        else:
            b.tt(eng, outfn(kc), xs[:, kc, :], rstd[:, :N], ALU.mult, reads=[bxs, brstd], writes=(), joins=[bout])


def load_weight_bf16(b, ph, dst, bdst, src, nrow_chunks, ncols, gvec=None, stage=None, bstage=None, rows=128, gbuf=None):
    for kc in range(nrow_chunks):
        b.ld(stage[:rows, :ncols], src[kc * rows:(kc + 1) * rows, :], bstage)
        eng = "vector" if kc % 2 == 0 else "gpsimd"
        if gvec is not None:
            b.ts(eng, dst[:rows, kc, :ncols], stage[:rows, :ncols], gvec[:rows, kc:kc + 1], None, ALU.mult, None,
                 reads=[bstage, gbuf], writes=(), joins=[bdst])
        else:
            b.cp(eng, dst[:rows, kc, :ncols], stage[:rows, :ncols], reads=[bstage], writes=(), joins=[bdst])


def build(upto=9, dbg=()):
    nc = bass.Bass("TRN2", target_bir_lowering=False)
    st = contextlib.ExitStack()
    with st:
        b = B(nc, st)
        k = b.k
        xT = b.din("xT", [1024, S])
        xo = b.din("xo", [1024, TQP])
        w1 = b.din("w1", [1024, NW1])
        gv = b.din("gv", [128, 24])
        mu = b.din("mu", [128, 10])
        cst = b.din("cst", [128, S])
        snt = b.din("snt", [128, S])
        vec = b.din("vec", [128, 2, 8])
        wd = b.din("wd", [64, 256])
        wa = b.din("wa", [64, 256])
        wg = b.din("wg", [160, 256])
        wba = b.din("wba", [64, 1024])
        wbr = b.din("wbr", [256, 1024])
        wgate = b.din("wgate", [1024, 2048])
        bgate = b.din("bgate", [128, 16])
        wout = b.din("wout", [1024, 1024])
        wup = b.din("wup", [1024, 2 * DFF])
        cw = b.din("cw", [128, 44, 3])
        cb = b.din("cb", [128, 44])
        wdn = b.din("wdn", [DFF, 1024])
        msk = b.din("msk", [128, 5, 128])
        oT = b.dout("oT", [1024, 2048])
        dbo = {}
        for name, shape, dt in dbg:
            dbo[name] = b.dout("dbg_" + name, shape, dt)
        qk = b.dscr("qk", [6, 64, S], BF16)
        vv = b.dscr("vv", [3, 64, 128, 64], BF16)
        rw = b.dscr("rw", [1280, S], F32)
        dsc = b.dscr("dsc", [64, S], F32)
        cin = b.dscr("cin", [4 * 1024, TQP], BF16)
        cout = b.dscr("cout", [1024, TQP], BF16)
        cinR = b.dscr("cinR", [4 * 1024, TQP], BF16)
        coutR = b.dscr("coutR", [1024, TQP], BF16)
        Bqk, Bvv, Brw, Bdsc, Bcin, Bcout, BoT = [Buf(n) for n in "qk vv rw dsc cin cout oT".split()]
        BcinR, BcoutR = Buf("cinR"), Buf("coutR")
        ccsem = k.newsem("ccsem")
        ccsemR = k.newsem("ccsemR")
        onesm = b.sb(st, "onesm", [128, 128], BF16)
        bones = Buf("ones")
        b.memset("vector", onesm[:], 1.0 / 1024.0, bones)
        b.epst = b.sb(st, "epst", [128, 4], F32)
        b.beps = Buf("eps")
        b.memset("vector", b.epst[:, 0:1], 1e-6, b.beps)
        b.memset("vector", b.epst[:, 1:2], 64e-5, b.beps)
        b.memset("vector", b.epst[:, 2:3], 1e-24, b.beps)
        b.memset("vector", b.epst[:, 3:4], 0.0, b.beps)
        gvs = b.sb(st, "gvs", [128, 24], F32)
        bgvs = Buf("gvs")
        b.ld(gvs[:], gv, bgvs)
        mskf = b.sb(st, "mskf", [128, 5, 128], F32)
        mskb = b.sb(st, "mskb", [128, 5, 128], BF16)
        bmskf, bmsk = Buf("mskf"), Buf("msk")
        b.ld(mskf[:], msk, bmskf)
        b.cp("vector", mskb[:], mskf[:], [bmskf], [bmsk])

        with contextlib.ExitStack() as ph:
            W1 = b.sb(ph, "W1", [128, 8, NW1], BF16)
            bW1 = Buf("W1")
            wsts_ = [b.sb(ph, "wst%d" % i, [128, NW1], F32) for i in range(3)]
            bwsts_ = [Buf("wst%d" % i) for i in range(3)]
            for kc in range(8):
                w_, bw_ = wsts_[kc % 3], bwsts_[kc % 3]
                b.ld(w_[:, :], w1[kc * 128:(kc + 1) * 128, :], bw_)
                if kc % 2 == 0:
                    b.ts("vector", W1[:, kc, :], w_[:, :], gvs[:, kc:kc + 1], None, ALU.mult, None, [bw_, bgvs], (), joins=[bW1])
                else:
                    b.act(W1[:, kc, :], w_[:, :], AF.Copy, [bw_, bgvs], (), scale=gvs[:, kc:kc + 1], joins=[bW1])
            mus = b.sb(ph, "mus", [128, 10], F32)
            bmus = Buf("mus")
            b.ld(mus[:], mu, bmus)
            xs = [b.sb(ph, "xs%d" % i, [128, 8, 512], F32) for i in range(2)]
            bxs = [Buf("xs%d" % i) for i in range(2)]
            sq = b.sb(ph, "sq", [128, 8, 512], BF16)
            bsq = Buf("sq")
            rstd = b.sb(ph, "rstd", [128, 512], F32)
            brstd = Buf("rstd")
            hT = b.sb(ph, "hT", [128, 8, 2048], BF16)
            bhT = Buf("hT")
            cs4 = b.sb(ph, "cs4", [128, 2, 2048], F32)
            bcs4 = Buf("cs4")
            Pt = [b.sb(ph, "P%d" % i, [128, 513], F32) for i in range(10)]
            bP = [Buf("P%d" % i) for i in range(10)]
            for i in range(10):
                b.memset("gpsimd", Pt[i][:, 0:513], 0.0, bP[i])
            t1 = b.sb(ph, "t1", [128, 512], F32)
            t2 = b.sb(ph, "t2", [128, 512], F32)
            t3 = b.sb(ph, "t3", [128, 512], F32)
            bt1, bt2, bt3 = Buf("t1"), Buf("t2"), Buf("t3")
            oq = [b.sb(ph, "oq%d" % i, [128, 512], BF16) for i in range(2)]
            boq = [Buf("oq%d" % i) for i in range(2)]
            tr = b.sb(ph, "tr", [128, 512], F32)
            btr = Buf("tr")
            orw = [b.sb(ph, "orw%d" % i, [128, 512], F32) for i in range(3)]
            borw = [Buf("orw%d" % i) for i in range(3)]
            vst = [b.sb(ph, "vst%d" % i, [128, 8, 64], BF16) for i in range(2)]
            bvst = [Buf("vst%d" % i) for i in range(2)]
            nsub = 0
            noq = 0
            norw = 0
            nvst = 0
            for tt_ in range(4):
                for sub in range(4):
                    t0 = tt_ * 2048 + sub * 512
                    xb, bxb = xs[nsub % 2], bxs[nsub % 2]
                    nsub += 1
                    b.ld(xb[:], xT.rearrange("(k p) t -> p k t", p=128)[:, :, t0:t0 + 512], bxb)
                    rmsnorm_sub(b, xb[:], bxb, 512, sq, bsq, onesm, bones, rstd, brstd, 1e-6,
                                lambda kc, sub=sub: hT[:, kc, sub * 512:(sub + 1) * 512], bhT, sub == 0)
                T0_ = tt_ * 2048
                b.ld(cs4[:, 0, :], cst[:, T0_:T0_ + 2048], bcs4)
                b.ld(cs4[:, 1, :], snt[:, T0_:T0_ + 2048], bcs4, join=True)
                for sp_ in range(2):
                    subs = (2 * sp_, 2 * sp_ + 1)
                    for (c0, n, outs_) in ((0, 128, (0, 2)), (256, 128, (1, 3)), (512, 64, (4,)), (640, 64, (5,))):
                        pas = [b.psum() for _ in subs]
                        for kc in range(8):
                            for si, sub in enumerate(subs):
                                b.mm(pas[si][0][0:n, :], W1[:, kc, c0:c0 + n], hT[:, kc, sub * 512:(sub + 1) * 512], kc == 0, kc == 7,
                                     [bW1, bhT], pas[si][1])
                        pcs = [b.psum() for _ in subs]
                        for kc in range(8):
                            for si, sub in enumerate(subs):
                                b.mm(pcs[si][0][0:n, :], W1[:, kc, c0 + n:c0 + 2 * n], hT[:, kc, sub * 512:(sub + 1) * 512], kc == 0, kc == 7,
                                     [bW1, bhT], pcs[si][1])
                        for si, sub in enumerate(subs):
                            t0 = T0_ + sub * 512
                            ssl = slice(sub * 512, (sub + 1) * 512)
                            pa, bpa = pas[si]
                            pc, bpc = pcs[si]
                            b.tt("vector", t1[0:n, :], pa[0:n, :], cs4[0:n, 0, ssl], ALU.mult, [bpa, bcs4], [bt1])
                            b.cp("scalar", t2[0:n, :], pc[0:n, :], [bpc], [bt2])
                            b.tt("gpsimd", t3[0:n, :], t2[0:n, :], cs4[0:n, 1, ssl], ALU.mult, [bt2, bcs4], [bt3])
                            o, bo = oq[noq % 2], boq[noq % 2]
                            noq += 1
                            b.tt("vector", o[0:n, :], t1[0:n, :], t3[0:n, :], ALU.add, [bt1, bt3], [bo])
                            for oi, idx in enumerate(outs_):
                                b.stx(qk[idx, :, t0:t0 + 512], o[oi * 64:(oi + 1) * 64, :], bo, Bqk, join=True)
                for ct, (co, n) in enumerate(RW_TILES):
                    c0 = RW_C0 + co
                    pas = [b.psum() for _ in range(4)]
                    for kc in range(8):
                        for sub in range(4):
                            b.mm(pas[sub][0][0:n, :], W1[:, kc, c0:c0 + n], hT[:, kc, sub * 512:(sub + 1) * 512], kc == 0, kc == 7,
                                 [bW1, bhT], pas[sub][1])
                    P, bPc = Pt[ct], bP[ct]
                    for sub in range(4):
                        t0 = T0_ + sub * 512
                        pa, bpa = pas[sub]
                        b.cp("gpsimd", P[:n, 0:1], P[:n, 512:513], [bPc], [bPc])
                        b.cp("scalar", P[:n, 1:513], pa[0:n, :], [bpa], [bPc])
                        b.tt("vector", tr[:n, :], P[:n, 0:512], P[:n, 1:513], ALU.subtract, [bPc], [btr])
                        o, bo = orw[norw % 3], borw[norw % 3]
                        norw += 1
                        b.stt(o[:n, :], tr[:n, :], mus[:n, ct:ct + 1], P[:n, 1:513], ALU.mult, ALU.add, [btr, bmus, bPc], [bo])
                        b.stx(rw[ct * 128:ct * 128 + n, t0:t0 + 512], o[:n, :], bo, Brw, join=True)
                for g in range(3):
                    d = DIL[g]
                    for half in range(2):
                        pa, bpa = b.psum()
                        blks = []
                        for bi8 in range(8):
                            bi = half * 8 + bi8
                            if d == 1:
                                tok = slice(bi * 128, (bi + 1) * 128)
                                blk = tt_ * 16 + bi
                            elif d == 4:
                                s_, r = bi // 4, bi % 4
                                tok = slice(s_ * 512 + r, s_ * 512 + 512, 4)
                                blk = r * 16 + tt_ * 4 + s_
                            else:
                                r = bi
                                tok = slice(r, 2048, 16)
                                blk = r * 4 + tt_
                            blks.append(blk)
                            for kc in range(8):
                                b.mm(pa[:, bi8 * 64:(bi8 + 1) * 64], hT[:, kc, tok], W1[:, kc, V_C0 + g * 64:V_C0 + (g + 1) * 64],
                                     kc == 0, kc == 7, [bW1, bhT], bpa)
                        vs_, bvs_ = vst[nvst % 2], bvst[nvst % 2]
                        nvst += 1
                        b.cp("scalar" if half == 0 else "vector", vs_[:].rearrange("p a c -> p (a c)"), pa[:, :], [bpa], [bvs_])
                        for bi8 in range(8):
                            b.stx(vv[g, blks[bi8], :, :], vs_[:, bi8, :], bvs_, Bvv, join=True)
        k.barrier()
        if upto <= 1:
            finish(b, dbo, dbg, qk=qk, rw=rw, vv=vv, Bqk=Bqk, Brw=Brw, Bvv=Bvv)
            return nc
        zt = b.sb(st, "zt", [128, 16, 2], BF16)
        bzt = Buf("zt")
        b.memset("vector", zt[:], 0.0, bzt)
        b.stx(cin[0:1024, 0:2].rearrange("(c p) t -> p c t", p=128), zt[:, 0:8, :], bzt, Bcin, join=True)
        b.stx(cinR[0:1024, 0:2].rearrange("(c p) t -> p c t", p=128), zt[:, 8:16, :], bzt, BcinR, join=True)
        zp = b.sb(st, "zp", [128, 32, TQP - TQ], BF16)
        bzp = Buf("zp")
        b.memset("vector", zp[:], 0.0, bzp)
        b.stx(cin[:, TQ:TQP].rearrange("(c p) t -> p c t", p=128), zp[:], bzp, Bcin, join=True)
        b.stx(cinR[:, TQ:TQP].rearrange("(c p) t -> p c t", p=128), zp[:], bzp, BcinR, join=True)

        with contextlib.ExitStack() as ph:
            acc = b.sb(ph, "acc", [128, S], F32)
            bacc = Buf("acc")
            qT = b.sb(ph, "qT", [64, S], BF16)
            kT = b.sb(ph, "kT", [64, S], BF16)
            bqT, bkT = Buf("qT"), Buf("kT")
            vt = b.sb(ph, "vt", [128, 64, 128], BF16)
            bvt = Buf("vt")
            bvt1 = Buf("vt1")
            b.memset("gpsimd", vt[:, :, 64:128], 1.0, bvt1)
            pT = [b.sb(ph, "pT%d" % i, [128, 256], BF16) for i in range(4)]
            bpT = [Buf("pT%d" % i) for i in range(4)]
            for g in range(3):
                d = DIL[g]
                M = S // d
                nb = M // 128
                b.ld(qT[:], qk[2 * g], bqT, src=Bqk)
                b.ld(kT[:], qk[2 * g + 1], bkT, src=Bqk)
                b.ld(vt[:, :, 0:64], vv[g].rearrange("b p c -> p b c"), bvt, src=Bvv)
                for r in range(d):
                    po, bpo = None, None
                    for n in range(nb):
                        blk = r * nb + n
                        nq = 256 if n < nb - 1 else 128
                        base = r + d * 128 * n
                        kap = kT[:, base:base + d * 127 + 1:d]
                        qap = qT[:, base:base + d * (nq - 1) + 1:d]
                        ps, bps = b.psum()
                        b.mm(ps[:, 0:nq], kap, qap, True, True, [bkT, bqT], bps)
                        p, bp = pT[blk % 4], bpT[blk % 4]
                        b.act(p[:, 0:nq], ps[:, 0:nq], AF.Exp, reads=[bps], writes=[bp], scale=0.125)
                        b.tt("gpsimd" if blk % 2 == 0 else "vector", p[:, 0:nq], p[:, 0:nq],
                             mskb[:, 0:2, :].rearrange("p a c -> p (a c)")[:, 0:nq], ALU.mult, [bp, bmsk], [bp])
                        if n % 4 == 0:
                            po, bpo = b.psum()
                        oc = (n % 4) * 128
                        if n > 0:
                            pp, bpp = pT[(blk - 1) % 4], bpT[(blk - 1) % 4]
                            b.mm(po[:, oc:oc + 128], vt[:, blk - 1, :], pp[:, 128:256], True, False, [bvt, bvt1, bpp], bpo)
                            b.mm(po[:, oc:oc + 128], vt[:, blk, :], p[:, 0:128], False, True, [bvt, bvt1, bp], bpo)
                        else:
                            b.mm(po[:, oc:oc + 128], vt[:, blk, :], p[:, 0:128], True, True, [bvt, bvt1, bp], bpo)
                        if n % 4 == 3:
                            a0_ = r + d * 128 * (n - 3)
                            aap = acc[:, a0_:a0_ + d * 511 + 1:d]
                            if g == 0:
                                b.cp("vector", aap, po[:, :], [bpo], (), joins=[bacc])
                            else:
                                b.tt("vector", aap, po[:, :], aap, ALU.add, [bpo, bacc], (), joins=[bacc])
            den = b.sb(ph, "den", [64, S], F32)
            bden = Buf("den")
            b.stx(dsc, acc[64:128, :], bacc, Bdsc, q="sync", join=True)
            b.ld(den[:], dsc, bden, src=Bdsc)
            yat = b.sb(ph, "yat", [64, S], BF16)
            byp = [Buf("yat%d" % i) for i in range(16)]
            bdp = [Buf("den%d" % i) for i in range(16)]
            wbs = b.sb(ph, "wbs", [64, 1024], F32)
            wbb = b.sb(ph, "wbb", [64, 1024], BF16)
            bwbs, bwbb = Buf("wbs"), Buf("wbb")
            b.ld(wbs[:], wba, bwbs)
            b.cp("vector", wbb[:], wbs[:], [bwbs], [bwbb])
            ast = [b.sb(ph, "ast%d" % i, [128, 8, 512], BF16) for i in range(2)]
            bast = [Buf("ast%d" % i) for i in range(2)]

            def recip_piece(sub):
                sl = slice(sub * 512, (sub + 1) * 512)
                b.k.op("vector", lambda e: e.reciprocal(out=den[:, sl], in_=den[:, sl]), reads=[bden], writes=[bdp[sub]])
                b.tt("gpsimd" if sub % 2 == 0 else "vector", yat[:, sl], acc[0:64, sl], den[:, sl], ALU.mult, [bacc, bdp[sub]], [byp[sub]])

            recip_piece(0)
            for sub in range(16):
                if sub + 1 < 16:
                    recip_piece(sub + 1)
                a_, ba_ = ast[sub % 2], bast[sub % 2]
                for ct in range(8):
                    ps, bps = b.psum()
                    b.mm(ps[:, :], wbb[:, ct * 128:(ct + 1) * 128], yat[:, sub * 512:(sub + 1) * 512], True, True, [bwbb, byp[sub]], bps)
                    eng = ("scalar", "vector")[ct % 2]
                    if ct == 0:
                        b.cp(eng, a_[:, ct, :], ps[:, :], [bps], [ba_])
                    else:
                        b.cp(eng, a_[:, ct, :], ps[:, :], [bps], (), joins=[ba_])
                store_branch(b, cin, Bcin, a_, ba_, sub, 0)
        k.barrier()
        if upto <= 2:
            finish(b, dbo, dbg, cin=cin, Bcin=Bcin)
            return nc
        RG = [[0, 1, 2, 3], [4, 5, 6, 7]]
        if not SKIP_CC:
            k.raw("gpsimd", lambda e: e.collective_compute("ReduceScatter", ALU.add, replica_groups=RG, ins=[cin], outs=[cout]),
                  reads=[Bcin], writes=[Bcout], sem=ccsem)

        with contextlib.ExitStack() as ph:
            phase3(b, ph, dbo, rw, Brw, vec, wd, wa, wg, wbr, cinR, BcinR, mskb, bmsk)
        k.barrier()
        if upto <= 3:
            finish(b, dbo, dbg)
            return nc
        k.raw("gpsimd", lambda e: e.collective_compute("ReduceScatter", ALU.add, replica_groups=RG, ins=[cinR], outs=[coutR]),
              reads=[BcinR], writes=[BcoutR], sem=ccsemR)
        x1s = b.dscr("x1s", [1024, TQP], F32)
        h2s = b.dscr("h2s", [1024, TQP], BF16)
        Bx1s, Bh2s = Buf("x1s"), Buf("h2s")
        NTL = TQ // NT4
        with contextlib.ExitStack() as ph:
            sb = lambda name, shape, dt: b.sb(ph, name, shape, dt)
            wst4 = [sb("wst4%d" % i, [128, 2048], F32) for i in range(3)]
            bwst4 = [Buf("wst4%d" % i) for i in range(3)]
            Wg = sb("Wg", [128, 8, 2048], BF16)
            Wo = sb("Wo", [128, 8, 1024], BF16)
            bWg, bWo = Buf("Wg"), Buf("Wo")
            for kc in range(16):
                w_, bw_ = wst4[kc % 3], bwst4[kc % 3]
                if kc < 8:
                    b.ld(w_[:, :], wgate[kc * 128:(kc + 1) * 128, :], bw_)
                    if kc % 2 == 0:
                        b.ts("vector", Wg[:, kc, :], w_[:, :], gvs[:, kc:kc + 1], None, ALU.mult, None, [bw_, bgvs], (), joins=[bWg])
                    else:
                        b.act(Wg[:, kc, :], w_[:, :], AF.Copy, [bw_, bgvs], (), scale=gvs[:, kc:kc + 1], joins=[bWg])
                else:
                    b.ld(w_[:, 0:1024], wout[(kc - 8) * 128:(kc - 7) * 128, :], bw_)
                    b.cp("vector" if kc % 2 == 0 else "scalar", Wo[:, kc - 8, :], w_[:, 0:1024], [bw_], (), joins=[bWo])
            bgs = sb("bgs", [128, 16], F32)
            bbgs = Buf("bgs")
            b.ld(bgs[:], bgate, bbgs)
            xt = sb("xt4", [128, 8, NT4], F32)
            sq = sb("sq4", [128, 8, NT4], BF16)
            rstd = sb("rstd4", [128, NT4], F32)
            hb = sb("hb4", [128, 8, NT4], BF16)
            gt = sb("gt4", [128, 16, NT4], BF16)
            AR = sb("AR4", [128, 16, NT4], BF16)
            m1 = sb("m14", [128, 8, NT4], BF16)
            m2 = sb("m24", [128, 8, NT4], BF16)
            mg = sb("mg4", [128, 8, NT4], BF16)
            x1 = sb("x14", [128, 8, NT4], F32)
            h2 = sb("h24", [128, 8, NT4], BF16)
            bxt, bsq, brstd, bhb, bgt, bAR, bm1, bm2, bmg, bx1, bh2 = [Buf(n) for n in "xt sq rstd hb gt AR m1 m2 mg x1 h2".split()]
            gts = b.dscr("gts", [2048, TQP], BF16)
            Bgts = Buf("gts")
            for tl in range(NTL):
                c0 = tl * NT4
                csl = slice(c0, c0 + NT4)
                b.ld(xt[:], xo.rearrange("(k p) t -> p k t", p=128)[:, :, csl], bxt)
                rmsnorm_sub(b, xt[:], bxt, NT4, sq, bsq, onesm, bones, rstd, brstd, 1e-6, lambda kc: hb[:, kc, :], bhb, True)
                for ct in range(16):
                    ps, bps = b.psum()
                    for kc in range(8):
                        b.mm(ps[:, :NT4], Wg[:, kc, ct * 128:(ct + 1) * 128], hb[:, kc, :], kc == 0, kc == 7, [bWg, bhb], bps)
                    b.act(gt[:, ct, :], ps[:, :NT4], AF.Sigmoid, [bps, bbgs], [bgt] if ct == 0 else (), bias=bgs[:, ct:ct + 1],
                          joins=() if ct == 0 else [bgt])
                b.stx(gts.rearrange("(c p) t -> p c t", p=128)[:, :, csl], gt[:], bgt, Bgts, join=True)
            for tl in range(NTL):
                c0 = tl * NT4
                csl = slice(c0, c0 + NT4)
                b.ld(xt[:], xo.rearrange("(k p) t -> p k t", p=128)[:, :, csl], bxt)
                b.ld(gt[:], gts.rearrange("(c p) t -> p c t", p=128)[:, :, csl], bgt, src=Bgts)
                b.ld(AR[:, 0:8, :], cout.rearrange("(c p) t -> p c t", p=128)[:, :, csl], bAR, src=Bcout)
                b.ld(AR[:, 8:16, :], coutR.rearrange("(c p) t -> p c t", p=128)[:, :, csl], bAR, src=BcoutR, join=True)
                b.tt("vector", m1[:], gt[:, 0:8, :], AR[:, 0:8, :], ALU.mult, [bgt, bAR], [bm1])
                b.tt("gpsimd", m2[:], gt[:, 8:16, :], AR[:, 8:16, :], ALU.mult, [bgt, bAR], [bm2])
                b.tt("vector", mg[:], m1[:], m2[:], ALU.add, [bm1, bm2], [bmg])
                for ct in range(8):
                    ps, bps = b.psum()
                    for kc in range(8):
                        b.mm(ps[:, :NT4], Wo[:, kc, ct * 128:(ct + 1) * 128], mg[:, kc, :], kc == 0, kc == 7, [bWo, bmg], bps)
                    b.tt("vector", x1[:, ct, :], ps[:, :NT4], xt[:, ct, :], ALU.add, [bps, bxt], [bx1] if ct == 0 else (),
                         joins=() if ct == 0 else [bx1])
                rmsnorm_sub(b, x1[:], bx1, NT4, sq, bsq, onesm, bones, rstd, brstd, 1e-6, lambda kc: h2[:, kc, :], bh2, True)
                b.stx(x1s.rearrange("(k p) t -> p k t", p=128)[:, :, csl], x1[:], bx1, Bx1s, join=True)
                b.stx(h2s.rearrange("(k p) t -> p k t", p=128)[:, :, csl], h2[:], bh2, Bh2s, join=True)
        k.barrier()
        with contextlib.ExitStack() as ph:
            sb = lambda name, shape, dt: b.sb(ph, name, shape, dt)
            HW = DFF // 2
            wsts = [sb("wst5%d" % i, [128, HW], F32) for i in range(2)]
            bwsts = [Buf("wst5%d" % i) for i in range(2)]
            Wu = sb("Wu", [128, 8, 2 * DFF], BF16)
            Wd = sb("Wd", [128, 22, 1024], BF16)
            bWu, bWd = Buf("Wu"), Buf("Wd")
            nst = 0
            for q4 in range(4):
                for kc in range(8):
                    w_, bw_ = wsts[nst % 2], bwsts[nst % 2]
                    b.ld(w_[:, :], wup[kc * 128:(kc + 1) * 128, q4 * HW:(q4 + 1) * HW], bw_)
                    if nst % 2 == 0:
                        b.ts("vector", Wu[:, kc, q4 * HW:(q4 + 1) * HW], w_[:, :], gvs[:, 8 + kc:9 + kc], None, ALU.mult, None,
                             [bw_, bgvs], (), joins=[bWu])
                    else:
                        b.act(Wu[:, kc, q4 * HW:(q4 + 1) * HW], w_[:, :], AF.Copy, [bw_, bgvs], (), scale=gvs[:, 8 + kc:9 + kc], joins=[bWu])
                    nst += 1
            for kc in range(22):
                w_, bw_ = wsts[nst % 2], bwsts[nst % 2]
                b.ld(w_[:, 0:1024], wdn[kc * 128:(kc + 1) * 128, :], bw_)
                b.cp("vector" if nst % 2 == 0 else "scalar", Wd[:, kc, :], w_[:, 0:1024], [bw_], (), joins=[bWd])
                nst += 1
            cws = sb("cws", [128, 44, 3], F32)
            cbs = sb("cbs", [128, 44], F32)
            bcws = Buf("cws")
            b.ld(cws[:], cw, bcws)
            b.ld(cbs[:], cb, bcws, join=True)
            crr = sb("crr", [128, 44, 2], F32)
            bcrr = Buf("crr")
            b.memset("vector", crr[:], 0.0, bcrr)
            h2 = sb("h25", [128, 8, NT4], BF16)
            x1 = sb("x15", [128, 8, NT4], F32)
            act = sb("act5", [128, 22, NT4], BF16)
            ub = [sb("ub%d" % i, [128, NT4 + 2], F32) for i in range(4)]
            yc = [sb("yc%d" % i, [128, NT4], F32) for i in range(4)]
            sg = [sb("sg%d" % i, [128, NT4], F32) for i in range(2)]
            rstd = sb("rstd5", [128, NT4], F32)
            bh2, bx1, bact, brstd = Buf("h2"), Buf("x1"), Buf("act"), Buf("rstd")
            bub = [Buf() for i in range(4)]
            byc = [Buf() for i in range(4)]
            bsg = [Buf() for i in range(2)]
            nu = 0
            for tl in range(NTL):
                c0 = tl * NT4
                csl = slice(c0, c0 + NT4)
                b.ld(h2[:], h2s.rearrange("(k p) t -> p k t", p=128)[:, :, csl], bh2, src=Bh2s)
                b.ld(x1[:], x1s.rearrange("(k p) t -> p k t", p=128)[:, :, csl], bx1, src=Bx1s)
                for i in range(22):
                    ys = []
                    for which in range(2):
                        ct = i + 22 * which
                        u_, bu_ = ub[nu % 4], bub[nu % 4]
                        y_, by_ = yc[nu % 4], byc[nu % 4]
                        nu += 1
                        ps, bps = b.psum()
                        for kc in range(8):
                            b.mm(ps[:, :NT4], Wu[:, kc, ct * 128:(ct + 1) * 128], h2[:, kc, :], kc == 0, kc == 7, [bWu, bh2], bps)
                        b.cp("gpsimd", u_[:, 0:2], crr[:, ct, :], [bcrr], [bu_])
                        b.cp("scalar", u_[:, 2:NT4 + 2], ps[:, :NT4], [bps], (), joins=[bu_])
                        b.cp("gpsimd", crr[:, ct, :], u_[:, NT4:NT4 + 2], [bu_], [bcrr])
                        b.ts("gpsimd", y_[:], u_[:, 2:NT4 + 2], cws[:, ct, 2:3], cbs[:, ct:ct + 1], ALU.mult, ALU.add, [bu_, bcws], [by_])
                        b.stt(y_[:], u_[:, 1:NT4 + 1], cws[:, ct, 1:2], y_[:], ALU.mult, ALU.add, [bu_, bcws, by_], [by_])
                        b.stt(y_[:], u_[:, 0:NT4], cws[:, ct, 0:1], y_[:], ALU.mult, ALU.add, [bu_, bcws, by_], [by_])
                        ys.append((y_, by_))
                    s_, bs_ = sg[i % 2], bsg[i % 2]
                    b.act(s_[:], ys[0][0][:], AF.Silu, [ys[0][1]], [bs_])
                    b.tt("gpsimd", act[:, i, :], s_[:], ys[1][0][:], ALU.mult, [bs_, ys[1][1]], [bact] if i == 0 else (),
                         joins=() if i == 0 else [bact])
                for ct in range(8):
                    ps, bps = b.psum()
                    for i in range(22):
                        b.mm(ps[:, :NT4], Wd[:, i, ct * 128:(ct + 1) * 128], act[:, i, :], i == 0, i == 21, [bWd, bact], bps)
                    b.tt("vector", x1[:, ct, :], ps[:, :NT4], x1[:, ct, :], ALU.add, [bps, bx1], [bx1])
                b.tt("gpsimd", act[:, 0:8, :], x1[:], x1[:], ALU.mult, [bx1], [bact])
                ps, bps = b.psum()
                for kc in range(8):
                    b.mm(ps[:, :NT4], onesm[:], act[:, kc, :], kc == 0, kc == 7, [bact, bones], bps)
                b.act(rstd[:], ps[:, :NT4], AF.Ln, [bps, b.beps], [brstd], bias=b.epst[:, 0:1])
                b.act(rstd[:], rstd[:], AF.Exp, [brstd], [brstd], scale=-0.5)
                for kc in range(8):
                    b.stt(x1[:, kc, :], x1[:, kc, :], gvs[:, 16 + kc:17 + kc], rstd[:], ALU.mult, ALU.mult, [bx1, bgvs, brstd], [bx1])
                lo = 2 if tl == 0 else 0
                b.stx(oT.rearrange("(k p) t -> p k t", p=128)[:, :, c0 + lo - 2:c0 + NT4 - 2], x1[:, :, lo:NT4], bx1, BoT, join=True)
        k.barrier(retire=False)
        k.emit()
        return nc


def store_branch(b, cin, Bcin, a_, ba_, sub, rowoff):
    q = sub // 4
    col0 = 2 + (sub % 4) * 512
    r0 = q * 1024
    b.stx(cin[r0:r0 + 1024, col0:col0 + 512].rearrange("(c p) t -> p c t", p=128), a_[:, :, :], ba_, Bcin, join=True)
    if sub % 4 == 3 and q < 3:
        r1 = (q + 1) * 1024
        b.stx(cin[r1:r1 + 1024, 0:2].rearrange("(c p) t -> p c t", p=128), a_[:, :, 510:512], ba_, Bcin, join=True)


def finish(b, dbo, dbg, **kw):
    k = b.k
    outs = []
    for name, shape, dt in dbg:
        if name not in kw:
            continue
        src = kw[name]
        Bsrc = kw["B" + name]
        bo = Buf("dbg_" + name)
        k.dma("sync", dbo[name], src, Bsrc, bo, bo)
        outs.append(bo)
    for e in ("sync",):
        waits = k._collect(e, outs, (), ())
        k.ops[e].append((waits, None, None))
    k.emit()


def _consts():
    pos = np.arange(S, dtype=np.float32)
    inv = np.power(np.float32(500000.0), -np.arange(0, 16, 2, dtype=np.float32) / np.float32(16)).astype(np.float32)
    ang = pos[None, :] * inv[:, None]
    cs = np.ones((128, S), np.float32)
    sn = np.zeros((128, S), np.float32)
    for o in (0, 64):
        cs[o + 0:o + 8] = np.cos(ang)
        cs[o + 8:o + 16] = np.cos(ang)
        sn[o + 0:o + 8] = -np.sin(ang)
        sn[o + 8:o + 16] = np.sin(ang)
    i = np.arange(128)[:, None]
    j = np.arange(128)[None, :]
    msk = np.zeros((128, 5, 128), np.float32)
    msk[:, 0] = (i <= j)
    msk[:, 1] = (i >= j)
    msk[:, 2] = (i < j)
    msk[:, 3] = (i > j)
    msk[:, 4] = (i == j)
    return cs, sn, msk


def prep_inputs(inp):
    f = lambda a: np.ascontiguousarray(np.asarray(a, dtype=np.float32))
    x = f(inp["x"])
    w_in = f(inp["w_in"])[0]
    cs, sn, msk = _consts()
    perm = np.concatenate([np.arange(8, 16), np.arange(0, 8), np.arange(16, 64)])
    RB = 2304
    colvec = lambda v: f(v).reshape(-1, 128).T
    gv = np.concatenate([colvec(inp["norm_mix_g"][0]), colvec(inp["norm_ffn_g"][0]), colvec(inp["norm_final_g"])], 1)
    mu_all = f(inp["mu_shift"])[0]
    common = {
        "gv": f(gv), "cst": cs, "snt": sn, "msk": msk,
        "wgate": f(w_in[:, RB + 3360:]), "bgate": colvec(inp["b_gate"][0]),
        "wout": f(inp["w_out"])[0], "wup": f(inp["w_ffn_up"])[0],
        "cw": f(np.asarray(inp["conv_w"])[0, :, 0, :].T.reshape(44, 128, 3).transpose(1, 0, 2)),
        "cb": colvec(inp["conv_b"][0]), "wdn": f(inp["w_ffn_down"])[0],
    }
    maps = []
    for c in range(8):
        bb, j = c // 4, c % 4
        cols = []
        qcs = [np.arange((g * 4 + j) * 64, (g * 4 + j) * 64 + 64) for g in range(3)]
        kcs = [768 + qc for qc in qcs]
        cols += [qcs[0], qcs[1], qcs[0][perm], qcs[1][perm], kcs[0], kcs[1], kcs[0][perm], kcs[1][perm],
                 qcs[2], qcs[2][perm], kcs[2], kcs[2][perm]]
        for g in range(3):
            hq = (g * 4 + j) * 64
            cols.append(1536 + np.arange(hq, hq + 64))
        rsel = []
        for part in range(3):
            rsel.append(RB + part * 1024 + j * 256 + np.arange(256))
        rsel.append(RB + 3072 + np.arange(288))
        cols += rsel
        cols = np.concatenate(cols)
        assert cols.size == NW1
        mu_sel = mu_all[np.concatenate(rsel) - RB]
        mu_t = np.zeros((128, 10), np.float32)
        for ct, (co, n) in enumerate(RW_TILES):
            mu_t[:n, ct] = mu_sel[co:co + n]
        ch = slice(j * 256, (j + 1) * 256)
        pv = lambda v: f(v).reshape(-1)[ch].reshape(2, 128).T
        k_a = pv(inp["k_a"][0])
        vecs = np.stack([pv(inp["w0"][0]), pv(inp["a0"][0]), pv(inp["k_k"][0]), k_a, pv(inp["r_k"][0]),
                         pv(inp["ln_x_w"][0]), pv(inp["ln_x_b"][0]), np.zeros_like(k_a)], axis=2)
        xTb = np.ascontiguousarray(x[bb].T)
        xo = np.zeros((1024, TQP), np.float32)
        lo = j * 2048 - 2
        if lo < 0:
            xo[:, 2:TQ] = xTb[:, 0:2048]
        else:
            xo[:, 0:TQ] = xTb[:, lo:lo + TQ]
        m = dict(common)
        m.update({
            "xT": xTb, "xo": xo, "w1": f(w_in[:, cols]), "mu": mu_t, "vec": f(vecs),
            "wd": f(inp["w_decay_up"][0][:, ch]), "wa": f(inp["w_a_up"][0][:, ch]), "wg": f(inp["w_g_up"][0][:, ch]),
            "wba": f(inp["w_branch_attn"][0][j * 64:(j + 1) * 64, :]),
            "wbr": f(inp["w_branch_rwkv"][0][ch, :]),
        })
        maps.append(m)
    return maps


_NC = None


def kernel(**inputs):
    global _NC
    if _NC is None:
        _NC = build()
    maps = prep_inputs(inputs)
    res = run_bass_kernel_spmd(_NC, maps, core_ids=list(range(8)))
    out = np.zeros((2, S, 1024), np.float32)
    for c in range(8):
        bb, j = c // 4, c % 4
        out[bb, j * 2048:(j + 1) * 2048, :] = res.results[c]["oT"].T
    return out


SKIP_CC = False
NTS = 1024
NCH = NTS // 128
C0 = 0.6065306597126334


def phase3(b, ph, dbo, rw, Brw, vec, wd, wa, wg, wbr, cin, Bcin, mskb, bmsk):
    k = b.k
    sb = lambda name, shape, dt: b.sb(ph, name, shape, dt)
    ident = mskb[:, 4, :]
    vecs = sb("vecs", [128, 2, 8], F32)
    bvecs = Buf("vecs")
    b.ld(vecs[:], vec, bvecs)
    b.ts("vector", vecs[:, :, 7], vecs[:, :, 3], -1.0, 1.0, ALU.mult, ALU.add, [bvecs], [bvecs])
    wstg = sb("wstg", [128, 1024], F32)
    bwstg = Buf("wstg")
    wdb = sb("wdb", [64, 256], BF16)
    wab = sb("wab", [64, 256], BF16)
    wgb = sb("wgb", [128, 2, 256], BF16)
    wbrb = sb("wbrb", [128, 2, 1024], BF16)
    bwdb, bwab, bwgb, bwbrb = Buf("wdb"), Buf("wab"), Buf("wgb"), Buf("wbrb")
    b.ld(wstg[0:64, 0:256], wd, bwstg)
    b.cp("vector", wdb[:], wstg[0:64, 0:256], [bwstg], [bwdb])
    b.ld(wstg[0:64, 0:256], wa, bwstg)
    b.cp("vector", wab[:], wstg[0:64, 0:256], [bwstg], [bwab])
    b.ld(wstg[:, 0:256], wg[0:128, :], bwstg)
    b.cp("vector", wgb[:, 0, :], wstg[:, 0:256], [bwstg], [bwgb])
    b.ld(wstg[0:32, 0:256], wg[128:160, :], bwstg)
    b.cp("vector", wgb[0:32, 1, :], wstg[0:32, 0:256], [bwstg], (), joins=[bwgb])
    for kc in range(2):
        b.ld(wstg[:, :], wbr[kc * 128:(kc + 1) * 128, :], bwstg)
        b.cp("vector", wbrb[:, kc, :], wstg[:, :], [bwstg], (), joins=[bwbrb])
    bd = sb("bd", [128, 128], BF16)
    bbd = Buf("bd")
    b.memset("vector", bd[:], 0.0, bbd)
    b.memset("vector", bd[0:64, 0:64], 1.0, bbd)
    b.memset("vector", bd[64:128, 64:128], 1.0, bbd)
    mk2 = sb("mk2", [128, 256], BF16)
    bmk2 = Buf("mk2")
    b.cp("vector", mk2[:, 0:128], mskb[:, 2, :], [bmsk], [bmk2])
    b.cp("vector", mk2[:, 128:256], mskb[:, 0, :], [bmsk], (), joins=[bmk2])
    rmask = sb("rmask", [128, NTS], F32)
    brmask = Buf("rmask")
    b.memset("vector", rmask[:], 1.0, brmask)
    b.memset("vector", rmask[:, 0:NTS:128], 0.0, brmask)
    yrw = sb("yrw", [128, 2, S], BF16)
    byrw = Buf("yrw")
    names = "R Kb Vb Ab CS KK Em Ev TA TB".split()
    A = {n: sb(n, [128, NTS], F32) for n in names}
    Bf = {n: Buf(n) for n in names}
    zg = sb("zg", [128, 2, NTS], F32)
    bzg = Buf("zg")
    tw = sb("tw", [64, NTS], BF16)
    zab = sb("zab", [64, NTS], BF16)
    sgb = sb("sgb", [128, 2, NTS], BF16)
    sqb = sb("sqb", [128, NTS], BF16)
    btw, bzab, bsgb, bsqb = [Buf(n) for n in "tw zab sgb sqb".split()]

    class Cx:
        pass
    cxs = []
    for i in range(2):
        cx = Cx()
        cx.ar = sb("ar%d" % i, [128, NCH, 256], BF16)
        cx.kt = sb("kt%d" % i, [128, NTS], BF16)
        cx.bt = sb("bt%d" % i, [128, NTS], BF16)
        cx.vb = sb("vb%d" % i, [128, NTS], BF16)
        cx.Ep = sb("Ep%d" % i, [128, NTS], F32)
        cx.YF = sb("YF%d" % i, [128, NTS], F32)
        cx.BS = sb("BS%d" % i, [128, NTS], F32)
        cx.G = sb("G%d" % i, [128, NTS], F32)
        cx.bar, cx.bkt, cx.bbt, cx.bvb, cx.bEp, cx.bYF, cx.bBS, cx.bG = [Buf() for _ in range(8)]
        cxs.append(cx)
    Sst = sb("Sst", [128, 64], F32)
    Sbf = sb("Sbf", [128, 64], BF16)
    tmpS = sb("tmpS", [128, 64], F32)
    bSst, bSbf, btmpS = Buf("Sst"), Buf("Sbf"), Buf("tmpS")
    NSL = 4
    tok = [sb("tok%d" % i, [128, 3, 128], BF16) for i in range(NSL)]
    btok = [Buf("tok%d" % i) for i in range(NSL)]
    LM2 = [sb("LM%d" % i, [128, 2, 256], BF16) for i in range(NSL)]
    LK2 = [sb("LK%d" % i, [128, 2, 256], BF16) for i in range(NSL)]
    Xb2 = [sb("Xb%d" % i, [128, 256], BF16) for i in range(NSL)]
    bLM2 = [Buf() for i in range(NSL)]
    bLK2 = [Buf() for i in range(NSL)]
    bXb2 = [Buf() for i in range(NSL)]
    AA2 = [[sb("AA%d%d" % (sl, i), [128, 512], BF16) for i in range(2)] for sl in range(NSL)]
    bAA2 = [[Buf() for i in range(2)] for sl in range(NSL)]
    Xs2 = sb("Xs2", [128, 128], BF16)
    Us2 = sb("Us2", [128, 128], BF16)
    bXs2, bUs2 = Buf("Xs2"), Buf("Us2")
    mkx = sb("mkx", [128, 512], BF16)
    mlx = sb("mlx", [128, 256], BF16)
    idx2 = sb("idx2", [128, 256], BF16)
    bmkx = Buf("mkx")
    b.cp("vector", mkx[:, 0:128], mskb[:, 2, :], [bmsk], [bmkx])
    b.cp("vector", mkx[:, 128:256], mskb[:, 2, :], [bmsk], (), joins=[bmkx])
    b.cp("vector", mkx[:, 256:384], mskb[:, 0, :], [bmsk], (), joins=[bmkx])
    b.cp("vector", mkx[:, 384:512], mskb[:, 0, :], [bmsk], (), joins=[bmkx])
    for h in range(2):
        b.cp("vector", mlx[:, h * 128:(h + 1) * 128], mskb[:, 3, :], [bmsk], (), joins=[bmkx])
        b.cp("vector", idx2[:, h * 128:(h + 1) * 128], mskb[:, 4, :], [bmsk], (), joins=[bmkx])
    ysb = sb("ysb", [128, 128], F32)
    bysb = Buf("ysb")
    stt6 = sb("stt6", [128, 2, 6], F32)
    mv = sb("mv", [128, 2, 2], F32)
    rs2 = sb("rs2", [128, 2], F32)
    yn = sb("yn", [128, 128], BF16)
    bst6, bmv, brs2, byn = Buf("st6"), Buf("mv"), Buf("rs2"), Buf("yn")

    def pieces():
        for i in range(NTS // 512):
            yield slice(i * 512, (i + 1) * 512)

    def prep(cx, p, stile):
        pc = slice(p * 128, (p + 1) * 128)
        t0 = stile * NTS
        tsl = slice(t0, t0 + NTS)
        cx.p = p
        cx.tsl = tsl
        ar, kt, bt, vb = cx.ar, cx.kt, cx.bt, cx.vb
        b.ld(A["R"][:], rw[p * 128:(p + 1) * 128, tsl], Bf["R"], src=Brw)
        b.ld(A["Kb"][:], rw[(2 + p) * 128:(3 + p) * 128, tsl], Bf["Kb"], src=Brw)
        b.ld(A["Vb"][:], rw[(4 + p) * 128:(5 + p) * 128, tsl], Bf["Vb"], src=Brw)
        b.ld(A["TA"][0:64, :], rw[768:832, tsl], Bf["TA"], src=Brw)
        b.ld(A["TB"][0:64, :], rw[896:960, tsl], Bf["TB"], src=Brw)
        b.ld(zg[:, 0, :], rw[1024:1152, tsl], bzg, src=Brw)
        b.ld(zg[0:32, 1, :], rw[1152:1184, tsl], bzg, src=Brw, join=True)
        yield
        b.act(tw[:], A["TA"][0:64, :], AF.Tanh, [Bf["TA"]], [btw])
        b.cp("gpsimd", zab[:], A["TB"][0:64, :], [Bf["TB"]], [bzab])
        b.act(sgb[:, 0, :], zg[:, 0, :], AF.Sigmoid, [bzg], [bsgb])
        b.act(sgb[0:32, 1, :], zg[0:32, 1, :], AF.Sigmoid, [bzg], (), joins=[bsgb])
        yield
        first = True
        for sl in pieces():
            ps, bps = b.psum()
            b.mm(ps[:, :], wdb[:, pc], tw[:, sl], True, True, [bwdb, btw], bps)
            b.act(A["TA"][:, sl], ps[:, :], AF.Sigmoid, [bps, bvecs], [Bf["TA"]] if first else (), bias=vecs[:, p, 0:1],
                  joins=() if first else [Bf["TA"]])
            ps, bps = b.psum()
            b.mm(ps[:, :], wab[:, pc], zab[:, sl], True, True, [bwab, bzab], bps)
            b.act(A["Ab"][:, sl], ps[:, :], AF.Sigmoid, [bps, bvecs], [Bf["Ab"]] if first else (), bias=vecs[:, p, 1:2],
                  joins=() if first else [Bf["Ab"]])
            ps, bps = b.psum()
            b.mm(ps[:, :], wgb[:, 0, pc], sgb[:, 0, sl], True, False, [bwgb, bsgb], bps)
            b.mm(ps[:, :], wgb[0:32, 1, pc], sgb[0:32, 1, sl], False, True, [bwgb, bsgb], bps)
            b.cp("vector", cx.G[:, sl], ps[:, :], [bps], [cx.bG] if first else (), joins=() if first else [cx.bG])
            first = False
            yield
        b.k.op("vector", lambda e: e.tensor_tensor_scan(out=A["CS"][:], data0=rmask[:], data1=A["TA"][:], initial=0.0,
                                                        op0=ALU.mult, op1=ALU.add),
               reads=[brmask, Bf["TA"]], writes=[Bf["CS"]])
        yield
        b.act(cx.Ep[:], A["CS"][:], AF.Exp, [Bf["CS"]], [cx.bEp], scale=-C0)
        b.act(A["Em"][:], A["CS"][:], AF.Exp, [Bf["CS"]], [Bf["Em"]], scale=C0)
        b.tt("gpsimd", A["TB"][:], A["CS"][:], A["TA"][:], ALU.subtract, [Bf["CS"], Bf["TA"]], [Bf["TB"]])
        b.act(A["Ev"][:], A["TB"][:], AF.Exp, [Bf["TB"]], [Bf["Ev"]], scale=-C0)
        yield
        b.act(sqb[:], A["Kb"][:], AF.Square, [Bf["Kb"], bvecs], [bsqb], scale=vecs[:, p, 2:3])
        first = True
        for sl in pieces():
            ps, bps = b.psum()
            b.mm(ps[:, :], bd[:], sqb[:, sl], True, True, [bbd, bsqb], bps)
            b.act(A["TA"][:, sl], ps[:, :], AF.Ln, [bps, b.beps], [Bf["TA"]] if first else (), bias=b.epst[:, 2:3],
                  joins=() if first else [Bf["TA"]])
            first = False
        yield
        b.act(A["TA"][:], A["TA"][:], AF.Exp, [Bf["TA"]], [Bf["TA"]], scale=-0.5)
        b.stt(A["KK"][:], A["Kb"][:], vecs[:, p, 2:3], A["TA"][:], ALU.mult, ALU.mult, [Bf["Kb"], bvecs, Bf["TA"]], [Bf["KK"]])
        yield
        b.ts("gpsimd", A["TB"][:], A["Ab"][:], vecs[:, p, 3:4], vecs[:, p, 7:8], ALU.mult, ALU.add, [Bf["Ab"], bvecs], [Bf["TB"]])
        b.tt("gpsimd", A["Kb"][:], A["Kb"][:], A["TB"][:], ALU.mult, [Bf["Kb"], Bf["TB"]], [Bf["Kb"]])
        yield
        b.tt("vector", A["TB"][:], A["KK"][:], A["Ab"][:], ALU.mult, [Bf["KK"], Bf["Ab"]], [Bf["TB"]])
        b.tt("vector", bt[:], A["TB"][:], A["Em"][:], ALU.mult, [Bf["TB"], Bf["Em"]], [cx.bbt])
        b.stt(ar[:, :, 0:128], A["KK"][:].rearrange("p (c t) -> p c t", t=128), -1.0,
              A["Ev"][:].rearrange("p (c t) -> p c t", t=128), ALU.mult, ALU.mult, [Bf["KK"], Bf["Ev"]], [cx.bar])
        b.tt("gpsimd", ar[:, :, 128:256], A["R"][:].rearrange("p (c t) -> p c t", t=128),
             cx.Ep[:].rearrange("p (c t) -> p c t", t=128), ALU.mult, [Bf["R"], cx.bEp], (), joins=[cx.bar])
        yield
        b.tt("gpsimd", kt[:], A["Kb"][:], A["Em"][:], ALU.mult, [Bf["Kb"], Bf["Em"]], [cx.bkt])
        b.cp("scalar", vb[:], A["Vb"][:], [Bf["Vb"]], [cx.bvb])
        yield
        b.tt("vector", A["TB"][:], A["R"][:], A["Kb"][:], ALU.mult, [Bf["R"], Bf["Kb"]], [Bf["TB"]])
        b.ts("vector", sqb[:], A["TB"][:], vecs[:, p, 4:5], None, ALU.mult, None, [Bf["TB"], bvecs], [bsqb])
        first = True
        for sl in pieces():
            ps, bps = b.psum()
            b.mm(ps[:, :], bd[:], sqb[:, sl], True, True, [bbd, bsqb], bps)
            b.tt("vector", cx.BS[:, sl], ps[:, :], A["Vb"][:, sl], ALU.mult, [bps, Bf["Vb"]], [cx.bBS] if first else (),
                 joins=() if first else [cx.bBS])
            first = False

    def pre(cx, c, par):
        ar, kt, bt, vb = cx.ar, cx.kt, cx.bt, cx.vb
        bar, bkt, bbt, bvb = cx.bar, cx.bkt, cx.bbt, cx.bvb
        csl = slice(c * 128, (c + 1) * 128)
        HS = [slice(0, 64), slice(64, 128)]
        ps, bps = b.psum()
        b.mm(ps[:, 0:128], vb[:, csl], ident, True, True, [bvb, bmsk], bps)
        b.mm(ps[:, 128:256], kt[:, csl], ident, True, True, [bkt, bmsk], bps)
        b.mm(ps[:, 256:384], bt[:, csl], ident, True, True, [bbt, bmsk], bps)
        b.cp("scalar", tok[par][:].rearrange("p a c -> p (a c)"), ps[:, 0:384], [bps], [btok[par]])
        yield
        for h in range(2):
            ps, bps = b.psum()
            b.mm(ps[:, 0:256], bt[HS[h], csl], ar[HS[h], c, :], True, True, [bbt, bar], bps)
            b.tt("vector", LM2[par][:, h, :], ps[:, 0:256], mk2[:], ALU.mult, [bps, bmk2], [bLM2[par]] if h == 0 else (),
                 joins=() if h == 0 else [bLM2[par]])
            ps, bps = b.psum()
            b.mm(ps[:, 0:256], kt[HS[h], csl], ar[HS[h], c, :], True, True, [bkt, bar], bps)
            b.tt("vector", LK2[par][:, h, :], ps[:, 0:256], mk2[:], ALU.mult, [bps, bmk2], [bLK2[par]] if h == 0 else (),
                 joins=() if h == 0 else [bLK2[par]])
            ps, bps = b.psum()
            b.mm(ps[:, 0:128], ar[HS[h], c, 0:128], bt[HS[h], csl], True, True, [bbt, bar], bps)
            b.tt("vector", AA2[par][0][:, 256 + h * 128:256 + (h + 1) * 128], ps[:, 0:128], mskb[:, 3, :], ALU.mult, [bps, bmsk],
                 [bAA2[par][0]] if h == 0 else (), joins=() if h == 0 else [bAA2[par][0]])
            b.tt("gpsimd", Xb2[par][:, h * 128:(h + 1) * 128], LM2[par][:, h, 0:128], ident, ALU.add, [bLM2[par], bmsk],
                 [bXb2[par]] if h == 0 else (), joins=() if h == 0 else [bXb2[par]])
        yield
        for lev in range(1, 7):
            src, bsrc = AA2[par][(lev - 1) % 2], bAA2[par][(lev - 1) % 2]
            dst, bdst = AA2[par][lev % 2], bAA2[par][lev % 2]
            ps, bps = b.psum()
            for h in range(2):
                if lev == 1:
                    A_ = LM2[par][:, h, 0:128]
                    rdA = [bsrc, bLM2[par]]
                else:
                    A_ = src[:, h * 128:(h + 1) * 128]
                    rdA = [bsrc]
                AT_ = src[:, 256 + h * 128:256 + (h + 1) * 128]
                if lev < 6:
                    b.mm(ps[:, h * 128:(h + 1) * 128], AT_, A_, True, True, rdA, bps)
                b.mm(ps[:, 256 + h * 128:256 + (h + 1) * 128], A_, AT_, True, True, rdA, bps)
            if lev < 6:
                b.cp("scalar", dst[:, :], ps[:, 0:512], [bps], [bdst])
            else:
                b.cp("scalar", dst[:, 256:512], ps[:, 256:512], [bps], [bdst])
            ps2, bps2 = b.psum()
            for h in range(2):
                b.mm(ps2[:, h * 128:(h + 1) * 128], dst[:, 256 + h * 128:256 + (h + 1) * 128], Xb2[par][:, h * 128:(h + 1) * 128], True, True,
                     [bdst, bXb2[par]], bps2)
            b.tt("vector", Xb2[par][:], ps2[:, 0:256], Xb2[par][:], ALU.add, [bps2, bXb2[par]], [bXb2[par]])
            yield

    SP = sb("SPst", [128, 64], F32)
    bSP = Buf("SP")

    def stateg(cx, c, par):
        ar, bar = cx.ar, cx.bar
        pcol = slice(c * 128 + 127, c * 128 + 128)
        psY, bpsY = b.pt[5 + (c % 2)], b.pb[5 + (c % 2)]
        psS, bpsS = b.pt[7], b.pb[7]
        HS = [slice(0, 64), slice(64, 128)]
        b.ts("gpsimd", SP[:, :], Sst[:, :], cx.Ep[:, pcol], None, ALU.mult, None, [bSst, cx.bEp], [bSP])
        for h in range(2):
            hs = HS[h]
            ps, bps = b.psum()
            b.mm(ps[:, 0:64], ar[hs, c, 0:128], Sbf[hs, :], True, False, [bar, bSbf], bps)
            b.mm(ps[:, 0:64], LK2[par][:, h, 0:128], tok[par][:, 0, hs], False, True, [bLK2[par], btok[par]], bps)
            b.cp("scalar", Xs2[:, hs], ps[:, 0:64], [bps], [bXs2] if h == 0 else (), joins=() if h == 0 else [bXs2])
        yield
        ps, bps = b.psum()
        for h in range(2):
            b.mm(ps[:, HS[h]], Xb2[par][:, h * 128:(h + 1) * 128], Xs2[:, h * 64:(h + 1) * 64], True, True, [bXb2[par], bXs2], bps)
        b.cp("scalar", Us2[:], ps[:, 0:128], [bps], [bUs2])
        yield
        for h in range(2):
            hs = HS[h]
            b.mm(psS[:, hs], tok[par][:, 2, :], Us2[:, h * 64:(h + 1) * 64], True, False, [btok[par], bUs2], bpsS)
            b.mm(psS[:, hs], tok[par][:, 1, :], tok[par][:, 0, hs], False, True, [btok[par]], bpsS)
        for h in range(2):
            hs = HS[h]
            b.mm(psY[:, hs], ar[hs, c, 128:256], Sbf[hs, :], True, False, [bar, bSbf], bpsY)
            b.mm(psY[:, hs], LM2[par][:, h, 128:256], Us2[:, h * 64:(h + 1) * 64], False, False, [bLM2[par], bUs2], bpsY)
            b.mm(psY[:, hs], LK2[par][:, h, 128:256], tok[par][:, 0, hs], False, True, [bLK2[par], btok[par]], bpsY)
        for h in range(2):
            hs = HS[h]
            b.stt(Sbf[hs, :], psS[hs, hs], cx.Ep[hs, pcol], SP[hs, :], ALU.mult, ALU.add, [bpsS, cx.bEp, bSP],
                  [bSbf] if h == 0 else (), joins=() if h == 0 else [bSbf])
        for h in range(2):
            hs = HS[h]
            b.stt(Sst[hs, :], psS[hs, hs], cx.Ep[hs, pcol], SP[hs, :], ALU.mult, ALU.add, [bpsS, cx.bEp, bSP],
                  [bSst] if h == 0 else (), joins=() if h == 0 else [bSst])
        yield

    def outg(cx, c):
        p = cx.p
        csl = slice(c * 128, (c + 1) * 128)
        psY, bpsY = b.pt[5 + (c % 2)], b.pb[5 + (c % 2)]
        for h in range(2):
            b.k.op("vector", lambda e, h=h: e.bn_stats(out=stt6[:, h, :], in_=psY[:, h * 64:(h + 1) * 64]),
                   reads=[bpsY], writes=[bst6] if h == 0 else (), joins=() if h == 0 else [bst6])
        for h in range(2):
            b.k.op("vector", lambda e, h=h: e.bn_aggr(out=mv[:, h, :], in_=stt6[:, h, :]),
                   reads=[bst6], writes=[bmv] if h == 0 else (), joins=() if h == 0 else [bmv])
        b.act(rs2[:], mv[:, :, 1], AF.Ln, [bmv, b.beps], [brs2], bias=b.epst[:, 1:2])
        b.act(rs2[:], rs2[:], AF.Exp, [brs2], [brs2], scale=-0.5)
        yield
        for h in range(2):
            b.ts("vector", yn[:, h * 64:(h + 1) * 64], psY[:, h * 64:(h + 1) * 64], mv[:, h, 0:1], rs2[:, h:h + 1],
                 ALU.subtract, ALU.mult, [bpsY, bmv, brs2], [byn] if h == 0 else (), joins=() if h == 0 else [byn])
        ps, bps = b.psum()
        b.mm(ps[:, 0:128], yn[:], ident, True, True, [byn, bmsk], bps)
        b.act(cx.YF[:, csl], ps[:, 0:128], AF.Identity, [bps, bvecs], [cx.bYF] if c == 0 else (),
              bias=vecs[:, p, 6:7], scale=vecs[:, p, 5:6], joins=() if c == 0 else [cx.bYF])
        yield

    def finalize(cx):
        b.tt("vector", cx.YF[:], cx.YF[:], cx.BS[:], ALU.add, [cx.bYF, cx.bBS], [cx.bYF])
        b.tt("gpsimd", yrw[:, cx.p, cx.tsl], cx.YF[:], cx.G[:], ALU.mult, [cx.bYF, cx.bG], (), joins=[byrw])

    def step(g):
        try:
            next(g)
            return True
        except StopIteration:
            return False

    ast = [sb("rst%d" % i, [128, 8, 512], BF16) for i in range(2)]
    bast = [Buf("rst%d" % i) for i in range(2)]

    def rproj(sub):
        a_, ba_ = ast[sub % 2], bast[sub % 2]
        for ct in range(8):
            ps, bps = b.psum()
            for kc in range(2):
                b.mm(ps[:, :], wbrb[:, kc, ct * 128:(ct + 1) * 128], yrw[:, kc, sub * 512:(sub + 1) * 512], kc == 0, kc == 1, [bwbrb, byrw], bps)
            eng = ("scalar", "vector")[ct % 2]
            if ct == 0:
                b.cp(eng, a_[:, ct, :], ps[:, :], [bps], [ba_])
            else:
                b.cp(eng, a_[:, ct, :], ps[:, :], [bps], (), joins=[ba_])
        store_branch(b, cin, Bcin, a_, ba_, sub, 1024)

    b.nring = 5
    NCHT = S // 128
    AHEAD = 3
    QUOTA = 3
    for p in range(2):
        b.memset("vector", Sst[:], 0.0, bSst)
        b.memset("gpsimd", Sbf[:], 0.0, bSbf)
        pregens = {}
        prepgens = {}
        for r in range(-AHEAD, NCHT + 1):
            gens = []
            if 0 <= r < NCHT:
                gens.append([stateg(cxs[(r // NCH) % 2], r % NCH, r % NSL), 99])
            if 1 <= r <= NCHT:
                gens.append([outg(cxs[((r - 1) // NCH) % 2], (r - 1) % NCH), 99])
            for dd in range(1, AHEAD + 1):
                gc = r + dd
                if 0 <= gc < NCHT:
                    if gc not in pregens:
                        cxn = cxs[(gc // NCH) % 2]
                        if gc % NCH == 0:
                            st_ = gc // NCH
                            if st_ not in prepgens:
                                prepgens[st_] = prep(cxn, p, st_)
                            while step(prepgens[st_]):
                                pass
                        pregens[gc] = pre(cxn, gc % NCH, gc % NSL)
                    gens.append([pregens[gc], QUOTA])
            nst_ = (r + AHEAD) // NCH + 1
            if r >= 0 and nst_ < S // NTS and 1 <= (r % NCH) <= NCH - AHEAD - 1:
                if nst_ not in prepgens:
                    prepgens[nst_] = prep(cxs[nst_ % 2], p, nst_)
                gens.append([prepgens[nst_], 3])
            alive = True
            while alive:
                alive = False
                for g_ in gens:
                    if g_[1] > 0:
                        g_[1] -= 1
                        if step(g_[0]):
                            alive = True
                        else:
                            g_[1] = 0
            if r + 1 in pregens:
                while step(pregens[r + 1]):
                    pass
                del pregens[r + 1]
            if r >= 1 and (r - 1) % NCH == NCH - 1:
                finalize(cxs[((r - 1) // NCH) % 2])
                if p == 1:
                    st_done = (r - 1) // NCH
                    for sub in range(st_done * (NTS // 512), (st_done + 1) * (NTS // 512)):
                        rproj(sub)
    b.nring = 8
    if "yrw" in dbo:
        b.stx(dbo["yrw"], yrw[:], byrw, Buf("dbgyrw"), join=True)


def xupd_sb(b, ps2, bps2, X, bX):
    b.tt("vector", X[:], ps2[:, 0:128], X[:], ALU.add, [bps2, bX], [bX])
```
